# Optimizing a Trainium2 kernel written in Bass

```python
import jax, jax.numpy as jnp
from jax import lax
import numpy as np

D_MODEL = 1024
BATCH = 2
SEQ = 8192
DEPTH = 2

GRID_W = 64
CTX_LEN = 256
D_MIX = D_MODEL
W_F = D_MIX // 4
W_ATT = D_MIX // 4
W_CONV = D_MIX // 4
W_POOL = D_MIX // 4
HEAD_DIM = 64
N_HEADS = W_ATT // HEAD_DIM
N_KV_HEADS = N_HEADS // 2
GQA = N_HEADS // N_KV_HEADS
KV_DIM = N_KV_HEADS * HEAD_DIM
ATT_SCALE = HEAD_DIM ** -0.5
ROPE_FREQS = HEAD_DIM // 4
ROPE_THETA = 10000.0
Q_BLOCK = 128
F_HEADS = 4
F_DIM = W_F // F_HEADS
CONV_W = 3
POOL_WINDOWS = (2, 4, 8, 16)
POOL_DIM = W_POOL // len(POOL_WINDOWS)
EPS = 1e-6
SPLIT_WIDTHS = (W_ATT, KV_DIM, KV_DIM, W_ATT, W_F, W_F, W_CONV, W_CONV, W_CONV, W_CONV, W_POOL, W_POOL)
D_IN = sum(SPLIT_WIDTHS)

kernel_name = "hymba_style_parallel_hybrid_dit"


def _split_cols(p):
    return jnp.split(p, np.cumsum(SPLIT_WIDTHS)[:-1].tolist(), axis=-1)


def _rmsnorm(x, g):
    xf = x.astype(jnp.float32)
    y = xf * lax.rsqrt(jnp.mean(xf * xf, axis=-1, keepdims=True) + EPS)
    return (y * g.astype(jnp.float32)).astype(x.dtype)


def _axial_angles(n):
    n_rows = n // GRID_W
    row = jnp.repeat(jnp.arange(n_rows), GRID_W).astype(jnp.float32)
    col = jnp.tile(jnp.arange(GRID_W), n_rows).astype(jnp.float32)
    inv = ROPE_THETA ** (-jnp.arange(ROPE_FREQS, dtype=jnp.float32) / ROPE_FREQS)
    return row[:, None] * inv, col[:, None] * inv


def _rotate(x, ang):
    f = ang.shape[-1]
    cos = jnp.cos(ang)[None, :, None, :].astype(x.dtype)
    sin = jnp.sin(ang)[None, :, None, :].astype(x.dtype)
    x1, x2 = x[..., :f], x[..., f:]
    return jnp.concatenate([x1 * cos - x2 * sin, x2 * cos + x1 * sin], axis=-1)


def _axial_rope(x, ang_row, ang_col):
    half = HEAD_DIM // 2
    return jnp.concatenate([_rotate(x[..., :half], ang_row), _rotate(x[..., half:], ang_col)], axis=-1)


def _latent_attention(q, k, v, kc, vc):
    b, s = q.shape[0], q.shape[1]
    nblk = s // Q_BLOCK
    k_all = jnp.concatenate([kc, k], axis=1)
    v_all = jnp.concatenate([vc, v], axis=1)
    qb = q.reshape(b, nblk, Q_BLOCK, N_KV_HEADS, GQA, HEAD_DIM).transpose(1, 0, 2, 3, 4, 5)

    def one_block(qi):
        sc = jnp.einsum('bqkgd,bskd->bkgqs', qi, k_all, preferred_element_type=jnp.float32)
        p = jax.nn.softmax(sc, axis=-1).astype(v_all.dtype)
        return jnp.einsum('bkgqs,bskd->bqkgd', p, v_all)

    o = lax.map(one_block, qb)
    return o.transpose(1, 0, 2, 3, 4, 5).reshape(b, s, W_ATT)


def _context_attention(qc, kc, vc):
    b, n = qc.shape[0], qc.shape[1]
    qg = qc.reshape(b, n, N_KV_HEADS, GQA, HEAD_DIM)
    sc = jnp.einsum('bqkgd,bskd->bkgqs', qg, kc, preferred_element_type=jnp.float32)
    p = jax.nn.softmax(sc, axis=-1).astype(vc.dtype)
    return jnp.einsum('bkgqs,bskd->bqkgd', p, vc).reshape(b, n, W_ATT)


def _fourier_branch(u, z, w_fourier):
    b, n, _ = u.shape
    uf = u.astype(jnp.float32).reshape(b, n, F_HEADS, F_DIM)
    y = jnp.fft.fftn(uf, axes=(1, 3), norm='ortho').real.reshape(b, n, W_F).astype(u.dtype)
    return jax.nn.silu(z) * (y @ w_fourier)


def _conv_branch(bg, cg, hv, z, conv_w, conv_b):
    t = cg * hv
    tp = jnp.pad(t, ((0, 0), (1, 1), (0, 0)))
    y = tp[:, :-2] * conv_w[0] + tp[:, 1:-1] * conv_w[1] + tp[:, 2:] * conv_w[2] + conv_b
    return jax.nn.silu(z) * (bg * y)


def _pool_branch(u, z, pool_w, pool_scale):
    b, n, _ = u.shape
    cs = jnp.cumsum(u.astype(jnp.float32), axis=1)
    cs = jnp.concatenate([jnp.zeros((b, 1, W_POOL), jnp.float32), cs], axis=1)
    pos = jnp.arange(n)
    outs = []
    for i, w in enumerate(POOL_WINDOWS):
        left = w // 2
        right = w - 1 - left
        lo = jnp.clip(pos - left, 0, n)
        hi = jnp.clip(pos + right + 1, 0, n)
        sl = slice(i * POOL_DIM, (i + 1) * POOL_DIM)
        mean = (cs[:, hi, sl] - cs[:, lo, sl]) / (hi - lo).astype(jnp.float32)[None, :, None]
        d = (mean - u[..., sl].astype(jnp.float32)).astype(u.dtype)
        outs.append(d @ pool_w[i])
    y = jnp.concatenate(outs, axis=-1) * pool_scale
    return jax.nn.silu(z) * y


def _local_branches(parts, w_fourier, conv_w, conv_b, pool_w, pool_scale):
    _, _, _, _, u_f, z_f, b_c, c_c, h_c, z_c, u_p, z_p = parts
    o_f = _fourier_branch(u_f, z_f, w_fourier)
    o_c = _conv_branch(b_c, c_c, h_c, z_c, conv_w, conv_b)
    o_p = _pool_branch(u_p, z_p, pool_w, pool_scale)
    return o_f, o_c, o_p


def _layer(x, xc, c, c_ctx, ang_row, ang_col, w_mod, b_mod, norm_g, w_in, q_gain, k_gain,
           w_fourier, conv_w, conv_b, pool_w, pool_scale, w_out, ctx_out):
    b, s, _ = x.shape
    n_ctx = xc.shape[1]
    shift, scale, gate = jnp.split(jax.nn.silu(c) @ w_mod + b_mod, 3, axis=-1)
    shift_c, scale_c, gate_c = jnp.split(jax.nn.silu(c_ctx) @ w_mod + b_mod, 3, axis=-1)
    h = _rmsnorm(x, norm_g) * (1.0 + scale[:, None]) + shift[:, None]
    hc = _rmsnorm(xc, norm_g) * (1.0 + scale_c) + shift_c

    parts = _split_cols(h @ w_in)
    if ctx_out:
        parts_c = _split_cols(hc @ w_in)
        kc_raw, vc_raw = parts_c[1], parts_c[2]
    else:
        kc_raw, vc_raw = jnp.split(hc @ w_in[:, W_ATT:W_ATT + 2 * KV_DIM], 2, axis=-1)
    kc = _rmsnorm(kc_raw.reshape(b, n_ctx, N_KV_HEADS, HEAD_DIM), k_gain)
    vc = vc_raw.reshape(b, n_ctx, N_KV_HEADS, HEAD_DIM)

    q = _rmsnorm(parts[0].reshape(b, s, N_HEADS, HEAD_DIM), q_gain)
    k = _rmsnorm(parts[1].reshape(b, s, N_KV_HEADS, HEAD_DIM), k_gain)
    v = parts[2].reshape(b, s, N_KV_HEADS, HEAD_DIM)
    q = _axial_rope(q, ang_row, ang_col) * ATT_SCALE
    k = _axial_rope(k, ang_row, ang_col)
    o_att = jax.nn.silu(parts[3]) * _latent_attention(q, k, v, kc, vc)
    o_f, o_c, o_p = _local_branches(parts, w_fourier, conv_w, conv_b, pool_w, pool_scale)
    out = jnp.concatenate([o_f, o_att, o_c, o_p], axis=-1) @ w_out
    x_new = x + gate[:, None] * out

    if ctx_out:
        qc = _rmsnorm(parts_c[0].reshape(b, n_ctx, N_HEADS, HEAD_DIM), q_gain) * ATT_SCALE
        oc_att = jax.nn.silu(parts_c[3]) * _context_attention(qc, kc, vc)
        oc_f, oc_c, oc_p = _local_branches(parts_c, w_fourier, conv_w, conv_b, pool_w, pool_scale)
        out_c = jnp.concatenate([oc_f, oc_att, oc_c, oc_p], axis=-1) @ w_out
        xc = xc + gate_c * out_c
    return x_new, xc


def setup_inputs(seed: int = 0) -> dict:
    key = jax.random.key(seed)
    ks = jax.random.split(key, 20)
    f32 = jnp.float32
    nrm = lambda k, shape: jax.random.normal(k, shape, f32)
    return {
        "x": nrm(ks[0], (BATCH, SEQ, D_MODEL)),
        "c": nrm(ks[1], (BATCH, D_MODEL)),
        "ctx": nrm(ks[2], (BATCH, CTX_LEN, D_MODEL)),
        "c_ctx": nrm(ks[3], (D_MODEL,)),
        "w_mod": nrm(ks[4], (DEPTH, D_MODEL, 3 * D_MODEL)) * (0.5 * D_MODEL ** -0.5),
        "b_mod": nrm(ks[5], (DEPTH, 3 * D_MODEL)) * 0.02,
        "norm_g": 1.0 + 0.05 * nrm(ks[6], (DEPTH, D_MODEL)),
        "w_in": nrm(ks[7], (DEPTH, D_MODEL, D_IN)) * D_MODEL ** -0.5,
        "q_gain": 1.0 + 0.05 * nrm(ks[8], (DEPTH, HEAD_DIM)),
        "k_gain": 1.0 + 0.05 * nrm(ks[9], (DEPTH, HEAD_DIM)),
        "w_fourier": nrm(ks[10], (DEPTH, W_F, W_F)) * W_F ** -0.5,
        "conv_w": nrm(ks[11], (DEPTH, CONV_W, W_CONV)) * CONV_W ** -0.5,
        "conv_b": nrm(ks[12], (DEPTH, W_CONV)) * 0.02,
        "pool_w": nrm(ks[13], (DEPTH, len(POOL_WINDOWS), POOL_DIM, POOL_DIM)) * POOL_DIM ** -0.5,
        "pool_scale": 1.0 + 0.1 * nrm(ks[14], (DEPTH, W_POOL)),
        "w_out": nrm(ks[15], (DEPTH, D_MIX, D_MODEL)) * D_MIX ** -0.5,
    }


def reference(x, c, ctx, c_ctx, w_mod, b_mod, norm_g, w_in, q_gain, k_gain,
              w_fourier, conv_w, conv_b, pool_w, pool_scale, w_out):
    ang_row, ang_col = _axial_angles(x.shape[1])
    xc = ctx
    for l in range(DEPTH):
        x, xc = _layer(x, xc, c, c_ctx, ang_row, ang_col, w_mod[l], b_mod[l], norm_g[l], w_in[l],
                       q_gain[l], k_gain[l], w_fourier[l], conv_w[l], conv_b[l], pool_w[l],
                       pool_scale[l], w_out[l], ctx_out=(l < DEPTH - 1))
    return x
```

```python
import contextlib
import numpy as np
import ml_dtypes
import concourse.bass as bass
import concourse.mybir as mybir
from concourse.bass_utils import run_bass_kernel_spmd

F32 = mybir.dt.float32
BF16 = mybir.dt.bfloat16
U8 = mybir.dt.uint8
AF = mybir.ActivationFunctionType
ALU = mybir.AluOpType
AX = mybir.AxisListType

PE, ACT, DVE, POOL, SP = "tensor", "scalar", "vector", "gpsimd", "sync"
ENGS = [PE, ACT, DVE, POOL, SP]

D = 1024
SEQ = 8192
NCTX = 256
DEPTH = 2
TLOC = 2048
NT = 18
NTOK = NT * 128
EPS = 1e-6
HO = [0, 2, 1, 3]
GW = [512, 384, 256, 512, 512, 512, 512]
GOFF = [0, 512, 896, 1152, 1664, 2176, 2688]
DIN2 = 3200


class Op:
    __slots__ = ("eng", "fn", "deps", "dma", "idx", "sig", "sigval", "dsem", "dval", "prev_same_sem")

    def __init__(self, eng, fn, dma):
        self.eng = eng
        self.fn = fn
        self.dma = dma
        self.deps = set()
        self.sig = False
        self.sigval = None
        self.dsem = None
        self.dval = None
        self.prev_same_sem = None


class Prog:
    def __init__(self, nc, n_dma_sems=16, same_engine_sync=True):
        self.nc = nc
        self.ops = []
        self.per_eng = {e: [] for e in ENGS}
        self.last_w = {}
        self.readers = {}
        self.same_engine_sync = same_engine_sync
        self.n_dma_sems = n_dma_sems
        self.barrier_op = None
        self.dmas_since_barrier = []

    def op(self, eng, fn, r=(), w=(), dma=False):
        o = Op(eng, fn, dma)
        o.idx = len(self.ops)
        self.ops.append(o)
        self.per_eng[eng].append(o)
        for k in r:
            lw = self.last_w.get(k)
            if lw is not None:
                o.deps.add(lw)
        for k in w:
            lw = self.last_w.get(k)
            if lw is not None:
                o.deps.add(lw)
            for rd in self.readers.get(k, ()):
                o.deps.add(rd)
        for k in r:
            self.readers.setdefault(k, []).append(o.idx)
        for k in w:
            self.last_w[k] = o.idx
            self.readers[k] = []
        if self.barrier_op is not None:
            o.deps.add(self.barrier_op)
        o.deps.discard(o.idx)
        if dma:
            self.dmas_since_barrier.append(o.idx)
        return o

    def dma(self, eng, out, in_, r=(), w=(), **kw):
        return self.op(eng, lambda e: e.dma_start(out=out, in_=in_, **kw), r=r, w=w, dma=True)

    def barrier(self, scratch):
        o = self.op(DVE, lambda e: e.memset(scratch, 0.0))
        for e in ENGS:
            if self.per_eng[e]:
                lst = [x for x in self.per_eng[e] if x.idx != o.idx]
                if lst:
                    o.deps.add(lst[-1].idx)
        for d in self.dmas_since_barrier:
            o.deps.add(d)
        self.dmas_since_barrier = []
        self.barrier_op = o.idx
        self.last_w = {}
        self.readers = {}
        return o

    def emit(self, final_wait_ops=()):
        nc = self.nc
        ops = self.ops
        for o in ops:
            nd = set()
            for d in o.deps:
                y = ops[d]
                if y.eng == o.eng and not y.dma and not o.dma:
                    if o.eng == PE or not self.same_engine_sync:
                        continue
                if y.eng == o.eng and not y.dma and o.dma and o.eng == SP:
                    continue
                nd.add(d)
            o.deps = nd
        for o in ops:
            for d in o.deps:
                ops[d].sig = True
        for d in final_wait_ops:
            d.sig = True
        stack = contextlib.ExitStack()
        esem = {}
        for e in ENGS:
            esem[e] = stack.enter_context(nc.semaphore("cs_" + e))
        dsems = {}
        for e in ENGS:
            dsems[e] = [stack.enter_context(nc.semaphore("ds_%s_%d" % (e, i))) for i in range(self.n_dma_sems)] \
                if any(o.dma for o in self.per_eng[e]) else []
        for e in ENGS:
            cnt = 0
            di = 0
            last_on_sem = {}
            semcnt = {}
            for o in self.per_eng[e]:
                if o.dma:
                    o.sig = True
                    si = di % self.n_dma_sems
                    di += 1
                    o.dsem = dsems[e][si]
                    semcnt[si] = semcnt.get(si, 0) + 16
                    o.dval = semcnt[si]
                    o.prev_same_sem = last_on_sem.get(si)
                    last_on_sem[si] = o
                elif o.sig:
                    cnt += 1
                    o.sigval = cnt
        block = stack.enter_context(nc.Block())

        def make_body(e):
            def body(eng):
                waited = {}

                def wait(sem, val, key):
                    if waited.get(key, 0) >= val:
                        return
                    eng.wait_ge(sem, val)
                    waited[key] = val

                def wait_op(y):
                    if y.dma:
                        wait(y.dsem, y.dval, ("d", y.eng, id(y.dsem)))
                    else:
                        wait(esem[y.eng], y.sigval, ("c", y.eng))

                for o in self.per_eng[e]:
                    for d in sorted(o.deps):
                        wait_op(ops[d])
                    if o.dma and o.prev_same_sem is not None:
                        wait_op(o.prev_same_sem)
                    ins = o.fn(eng)
                    if o.dma:
                        ins.then_inc(o.dsem, 16)
                    elif o.sig:
                        ins.then_inc(esem[e], 1)
                if e == SP:
                    for o in final_wait_ops:
                        wait_op(o)
            return body

        for e in ENGS:
            if self.per_eng[e] or e == SP:
                getattr(block, e)(make_body(e))
        stack.close()


TAB_ID = 0
TAB_CCS = 128
TAB_W1 = TAB_CCS + 1024
TAB_A_END = TAB_W1 + 128
TB_T2C = 0
TB_T2S = 2048
TB_TCC = 4096
TB_POOL = 5120
TB_END = TB_POOL + 4 * 7 * 128
POOL_WINDOWS = (2, 4, 8, 16)


def _band(w, nseq, out_start, in_start):
    left = w // 2
    right = w - 1 - left
    m = np.zeros((128, 128), np.float64)
    for o in range(128):
        go = out_start + o
        if go < 0 or go >= nseq:
            continue
        lo = min(max(go - left, 0), nseq)
        hi = min(max(go + right + 1, 0), nseq)
        cnt = hi - lo
        for gi in range(lo, hi):
            i = gi - in_start
            if 0 <= i < 128:
                m[i, o] += 1.0 / cnt
        i = go - in_start
        if 0 <= i < 128:
            m[i, o] -= 1.0
    return m


_CONST_CACHE = {}


def _tables(j):
    if j in _CONST_CACHE:
        return _CONST_CACHE[j]
    ta = np.zeros((128, TAB_A_END), np.float64)
    ta[:, TAB_ID:TAB_ID + 128] = np.eye(128)
    cc = np.zeros((256, 512))
    c = np.arange(64)
    ang = 2 * np.pi * np.outer(c, c) / 64.0
    for h in range(4):
        cc[h * 64:(h + 1) * 64, h * 64:(h + 1) * 64] = np.cos(ang)
        cc[h * 64:(h + 1) * 64, 256 + h * 64:256 + (h + 1) * 64] = -np.sin(ang)
    ta[:, TAB_CCS:TAB_CCS + 1024] = cc.reshape(2, 128, 512).transpose(1, 0, 2).reshape(128, 1024)
    n2 = np.arange(64)
    a1 = 2 * np.pi * np.outer(n2, n2) / 64.0
    w1 = np.zeros((128, 128))
    w1[0:64, 0:64] = np.cos(a1)
    w1[64:128, 0:64] = np.sin(a1)
    w1[0:64, 64:128] = -np.sin(a1)
    w1[64:128, 64:128] = np.cos(a1)
    ta[:, TAB_W1:TAB_W1 + 128] = w1
    tb = np.zeros((128, TB_END), np.float64)
    n1 = np.arange(128)[:, None, None]
    k2 = np.arange(64)[None, :, None]
    k1 = (32 * j + np.arange(32))[None, None, :]
    kk = 64 * k1 + k2
    a2 = 2 * np.pi * ((kk * n1) % SEQ) / float(SEQ)
    sc = 1.0 / np.sqrt(SEQ * 64.0)
    tb[:, TB_T2C:TB_T2C + 2048] = (np.cos(a2) * sc).reshape(128, 2048)
    tb[:, TB_T2S:TB_T2S + 2048] = (np.sin(a2) * sc).reshape(128, 2048)
    n = (np.arange(2)[None, :, None] * 128 + np.arange(128)[:, None, None])
    k = np.arange(256)[None, None, :]
    ac = 2 * np.pi * ((n * k) % 256) / 256.0
    scc = 1.0 / np.sqrt(256 * 64.0)
    tcc = np.stack([np.cos(ac) * scc, np.sin(ac) * scc], axis=2)
    tb[:, TB_TCC:TB_TCC + 1024] = tcc.reshape(128, 1024)
    pt = np.zeros((128, 4, 7, 128))
    for wi, w in enumerate(POOL_WINDOWS):
        big = 1 << 20
        pt[:, wi, 0] = _band(w, big, 1280, 1152)
        pt[:, wi, 1] = _band(w, big, 1280, 1408)
        pt[:, wi, 2] = _band(w, big, 1280, 1280)
        pt[:, wi, 3] = _band(w, SEQ, 0, 0) if j == 0 else pt[:, wi, 2]
        pt[:, wi, 4] = _band(w, SEQ, SEQ - 128, SEQ - 128) if j == 3 else pt[:, wi, 2]
        pt[:, wi, 5] = _band(w, NCTX, 0, 0)
        pt[:, wi, 6] = _band(w, NCTX, 128, 128)
    tb[:, TB_POOL:TB_END] = pt.reshape(128, -1)
    inv = 10000.0 ** (-np.arange(16, dtype=np.float64) / 16.0)
    rope = np.zeros((128, 2, NT, 64), np.float64)
    rope[:, 0, :, :] = 1.0
    for t in range(16):
        tok = 2048 * j + t * 128 + np.arange(128)
        row = (tok // 64).astype(np.float64)[:, None] * inv
        col = (tok % 64).astype(np.float64)[:, None] * inv
        rope[:, 0, t, 0:16] = np.cos(row)
        rope[:, 0, t, 16:32] = np.cos(row)
        rope[:, 0, t, 32:48] = np.cos(col)
        rope[:, 0, t, 48:64] = np.cos(col)
        rope[:, 1, t, 0:16] = -np.sin(row)
        rope[:, 1, t, 16:32] = np.sin(row)
        rope[:, 1, t, 32:48] = -np.sin(col)
        rope[:, 1, t, 48:64] = np.sin(col)
    out = (ta.astype(ml_dtypes.bfloat16), tb.astype(ml_dtypes.bfloat16), rope.astype(np.float32))
    _CONST_CACHE[j] = out
    return out


def _swap_idx():
    d = np.arange(64)
    return np.where((d % 32) < 16, d + 16, d - 16)


def _win_perm():
    sw = _swap_idx()
    q0, k0, v0, za, uf, zf, bc, cc, hc, zc, up, zp = np.cumsum([0, 256, 128, 128, 256, 256, 256, 256, 256, 256, 256, 256])
    cols = []
    cols += [q0 + h * 64 + d for h in HO for d in range(64)]
    cols += [q0 + h * 64 + sw[d] for h in HO for d in range(64)]
    cols += [k0 + i for i in range(128)]
    cols += [k0 + kv * 64 + sw[d] for kv in range(2) for d in range(64)]
    cols += [v0 + i for i in range(128)]
    cols += [up + i for i in range(256)]
    for c in range(2):
        for base in (cc, hc, bc, zc):
            cols += [base + c * 128 + i for i in range(128)]
    cols += [za + h * 64 + d for h in HO for d in range(64)]
    cols += [zf + i for i in range(256)]
    cols += [uf + i for i in range(256)]
    cols += [zp + i for i in range(256)]
    assert len(cols) == DIN2
    return np.array(cols)


SC_C = 0
SC_BMOD = 16
SC_G = 40
SC_CW = 48
SC_CB = 54
SC_PS = 56
SC_SELL = 58
SC_SELR = 62
SC_QG = 66
SC_QGS = 322
SC_KG = 578
SC_KGS = 706
SC_END = 834


def _layer_host_inputs(l, b, j, inp):
    f32 = np.float32
    sw = _swap_idx()
    sc = np.zeros((128, SC_END), f32)
    cb = np.asarray(inp["c"][b], f32).reshape(8, 128).T
    cx = np.asarray(inp["c_ctx"], f32).reshape(8, 128).T
    sc[:, SC_C + 0:SC_C + 16:2] = cb
    sc[:, SC_C + 1:SC_C + 16:2] = cx
    sc[:, SC_BMOD:SC_BMOD + 24] = np.asarray(inp["b_mod"][l], f32).reshape(24, 128).T
    sc[:, SC_G:SC_G + 8] = np.asarray(inp["norm_g"][l], f32).reshape(8, 128).T
    cw = np.asarray(inp["conv_w"][l], f32)
    for c in range(2):
        for t in range(3):
            sc[:, SC_CW + c * 3 + t] = cw[t, c * 128:(c + 1) * 128]
    sc[:, SC_CB:SC_CB + 2] = np.asarray(inp["conv_b"][l], f32).reshape(2, 128).T
    sc[:, SC_PS:SC_PS + 2] = np.asarray(inp["pool_scale"][l], f32).reshape(2, 128).T
    if j > 0:
        sc[:, SC_SELL + j - 1] = 1.0
    if j < 3:
        sc[:, SC_SELR + j + 1] = 1.0
    qg = np.asarray(inp["q_gain"][l], f32)
    kg = np.asarray(inp["k_gain"][l], f32)
    sc[:, SC_QG:SC_QG + 256] = np.tile(qg, 4)[None, :]
    sc[:, SC_QGS:SC_QGS + 256] = np.tile(qg[sw], 4)[None, :]
    sc[:, SC_KG:SC_KG + 128] = np.tile(kg, 2)[None, :]
    sc[:, SC_KGS:SC_KGS + 128] = np.tile(kg[sw], 2)[None, :]
    bg = np.ascontiguousarray(np.broadcast_to(np.asarray(inp["b_mod"][l], f32)[2048:3072][None, :], (128, 1024)))
    w_in = np.ascontiguousarray(np.asarray(inp["w_in"][l], f32)[:, _win_perm()])
    pw = np.asarray(inp["pool_w"][l], f32)
    pwp = np.zeros((64, 4, 128), f32)
    for g in range(4):
        off = 64 * (g % 2)
        pwp[:, g, off:off + 64] = pw[g]
    w_out = np.asarray(inp["w_out"][l], f32)
    rows = np.arange(1024)
    rows[256:512] = np.array([256 + h * 64 + d for h in HO for d in range(64)])
    w_out = np.ascontiguousarray(w_out[rows])
    return {
        "sc": sc, "bg": bg, "w_mod": np.ascontiguousarray(np.asarray(inp["w_mod"][l], f32)),
        "w_in": w_in, "pwp": pwp, "w_four": np.ascontiguousarray(np.asarray(inp["w_fourier"][l], f32)),
        "w_out": w_out,
    }


ARENA_BYTES = 106 * 1024


def build_program(mode, dbg=False):
    nc = bass.Bass("TRN2", target_bir_lowering=False)

    def din(name, shape, dt=F32):
        return nc.dram_tensor(name, list(shape), dt, kind="ExternalInput").ap()

    def dout(name, shape, dt=F32):
        return nc.dram_tensor(name, list(shape), dt, kind="ExternalOutput").ap()

    x_in = din("x", [TLOC, D])
    xc_in = din("xc", [NCTX, D])
    sc_in = din("sc", [128, SC_END])
    bg_in = din("bg", [128, 1024])
    wmod_in = din("w_mod", [D, 3 * D])
    win_in = din("w_in", [D, DIN2])
    pwp_in = din("pwp", [64, 4, 128])
    wf_in = din("w_four", [256, 256])
    wo_in = din("w_out", [D, D])
    ta_in = din("tab_a", [128, TAB_A_END], BF16)
    tb_in = din("tab_b", [128, TB_END], BF16)
    rope_in = din("rope", [128, 2, NT, 64])
    idf_in = din("idf", [128, 128])
    if mode == "A":
        pK = dout("pK", [128, TLOC], BF16)
        pV = dout("pV", [TLOC, 128], BF16)
        pF = dout("pF", [4, 2, TLOC, 64], BF16)
        pUP = dout("pUP", [2, 128, 256], BF16)
        pT = dout("pT", [128, 4], BF16)
    else:
        gK = din("gK", [4, 128, TLOC], BF16)
        gV = din("gV", [4, TLOC, 128], BF16)
        gF = din("gF", [4, 4, 2, TLOC, 64], BF16)
        gUP = din("gUP", [4, 2, 128, 256], BF16)
        gT = din("gT", [4, 128, 4], BF16)
        x_out = dout("x_out", [TLOC, D])
        xc_out = dout("xc_out", [NCTX, D])
        cK = nc.dram_tensor("cK", [128, NCTX], BF16).ap()
        cV = nc.dram_tensor("cV", [NCTX, 128], BF16).ap()

    st = contextlib.ExitStack()
    with st:
        def sb(name, shape, dt):
            return st.enter_context(nc.sbuf_tensor(name, list(shape), dt))

        SC = sb("SC", [128, SC_END], F32)
        BG = sb("BG", [128, 1024], F32)
        TA = sb("TA", [128, TAB_A_END], BF16)
        IDF = sb("IDF", [128, 128], F32)
        GATE = sb("GATE", [128, 2, 1024], F32)
        MODT = sb("MODT", [128, 16, 2], F32)
        G1 = sb("G1", [128, 8, 2], F32)
        SILC = sb("SILC", [128, 16], F32)
        SILCB = sb("SILCB", [128, 16], BF16)
        OT = sb("OT", [128, 8, NTOK], BF16)
        QT = sb("QT", [128, 2, NTOK], BF16)
        TT = sb("TT", [128, 2, 2308], BF16)
        UP = sb("UP", [128, NT, 256], BF16)
        UCSC = sb("UCSC", [128, 2, 512], BF16)
        PWP = sb("PWP", [64, 4, 128], BF16)
        WF = sb("WF", [128, 2, 256], BF16)
        SMALL = sb("SMALL", [128, 64], F32)
        NHALF = sb("NHALF", [128, 8], F32)
        ONES = sb("ONES", [128, 128], F32)
        EDG = sb("EDG", [128, 4], BF16)
        ARENA = sb("ARENA", [128, ARENA_BYTES], U8)
        PS = st.enter_context(nc.psum_tensor("PS", [128, 8, 512], F32))

        cursor = [0]

        def arena_reset():
            cursor[0] = 0

        def av(shape, dt):
            n = int(np.prod(shape[1:])) * (2 if dt == BF16 else 4)
            n = (n + 63) // 64 * 64
            off = cursor[0]
            cursor[0] += n
            assert cursor[0] <= ARENA_BYTES, ("arena overflow", cursor[0])
            v = ARENA[0:shape[0], off:off + int(np.prod(shape[1:])) * (2 if dt == BF16 else 4)].bitcast(dt)
            if len(shape) == 3:
                v = v.rearrange("p (a b) -> p a b", b=shape[2])
            elif len(shape) == 4:
                v = v.rearrange("p (a b c) -> p a b c", b=shape[2], c=shape[3])
            return v

        P = Prog(nc)
        psb = lambda b: "ps%d" % b

        P.dma(SP, SC[:], sc_in, w=["SC"])
        P.dma(SP, BG[:], bg_in, w=["BG"])
        P.dma(SP, TA[:], ta_in, w=["TA"])
        P.dma(SP, IDF[:], idf_in, w=["IDF"])
        P.dma(POOL, PWP[:], pwp_in, w=["PWP"])
        P.dma(POOL, WF[:], wf_in.rearrange("(c p) n -> p c n", p=128), w=["WF"])
        P.op(DVE, lambda e: e.memset(NHALF[:], -0.5), w=["NHALF"])
        P.op(DVE, lambda e: e.memset(ONES[:], 1.0), w=["ONES"])
        P.op(DVE, lambda e: e.memset(TT[:], 0.0), w=["TT"])

        arena_reset()
        ROPE = av([128, 2, NT, 64], F32)
        HT = av([128, 8, NTOK], BF16)
        WG = [av([128, 8, 512], BF16) for _ in range(2)]
        XT = [av([128, 1024], F32) for _ in range(2)]
        JUNK = av([128, 1024], BF16)
        SREP = av([128, 2, 8, 128], BF16)
        EA = av([128, 256], F32)
        EB = av([128, 256], F32)
        EC = av([128, 256], F32)
        QR = [av([128, 256], BF16) for _ in range(2)]
        KR = [av([128, 128], BF16) for _ in range(2)]
        UFT = av([128, 2, 512], BF16)
        UCS = [av([128, 512], BF16) for _ in range(2)]
        KTS = av([128, NTOK], BF16)
        VS = av([128, NT, 128], BF16)
        CT1 = [av([128, 512], F32) for _ in range(2)]
        CT2 = [av([128, 512], F32) for _ in range(2)]
        arena_a_end = cursor[0]

        P.dma(SP, ROPE[:], rope_in, w=["ROPE"])

        P.op(ACT, lambda e: e.activation(out=SILC[:], in_=SC[:, SC_C:SC_C + 16], func=AF.Silu), r=["SC"], w=["SILC"])
        P.op(DVE, lambda e: e.tensor_copy(out=SILCB[:], in_=SILC[:]), r=["SILC"], w=["SILCB"])
        for v in range(2):
            for k in range(8):
                P.op(DVE, lambda e, v=v, k=k: e.tensor_scalar(
                    out=SREP[:, v, k, :], in0=ONES[:], scalar1=SILC[:, 2 * k + v:2 * k + v + 1], scalar2=None,
                    op0=ALU.mult), r=["SILC", "ONES"], w=["SREP"])
        wmod_v = wmod_in.rearrange("(k p) n -> p k n", p=128)
        for piece in range(6):
            wg = WG[piece % 2]
            wk = "WG%d" % (piece % 2)
            P.dma(POOL, wg[:], wmod_v[:, :, piece * 512:(piece + 1) * 512], w=[wk])
            if piece < 4:
                for oc in range(4):
                    ch = piece * 4 + oc
                    for k in range(8):
                        P.op(PE, lambda e, wg=wg, oc=oc, k=k: e.matmul(
                            PS[:, 0, oc * 2:oc * 2 + 2], lhsT=wg[:, k, oc * 128:(oc + 1) * 128],
                            rhs=SILCB[:, 2 * k:2 * k + 2], start=(k == 0), stop=(k == 7)),
                            r=[wk, "SILCB"], w=[psb(0)])
                    P.op(DVE, lambda e, oc=oc, ch=ch: e.tensor_scalar(
                        out=MODT[:, ch, :], in0=PS[:, 0, oc * 2:oc * 2 + 2], scalar1=SC[:, SC_BMOD + ch:SC_BMOD + ch + 1],
                        scalar2=None, op0=ALU.add), r=["SC"], w=["MODT", psb(0)])
            else:
                half = piece - 4
                for v in range(2):
                    for k in range(8):
                        P.op(PE, lambda e, wg=wg, v=v, k=k: e.matmul(
                            PS[:, 1 + v, :], lhsT=SREP[:, v, k, :], rhs=wg[:, k, :], start=(k == 0), stop=(k == 7)),
                            r=[wk, "SREP"], w=[psb(1 + v)])
                    P.op(DVE, lambda e, v=v, half=half: e.tensor_tensor(
                        out=GATE[:, v, half * 512:(half + 1) * 512], in0=PS[:, 1 + v, :],
                        in1=BG[:, half * 512:(half + 1) * 512], op=ALU.add), r=["BG"], w=["GATE", psb(1 + v)])
        for v in range(2):
            P.op(DVE, lambda e, v=v: e.scalar_tensor_tensor(
                out=G1[:, :, v], in0=MODT[:, 8:16, v], scalar=1.0, in1=SC[:, SC_G:SC_G + 8],
                op0=ALU.add, op1=ALU.mult), r=["MODT", "SC"], w=["G1"])

        win_v = win_in.rearrange("(k p) n -> p k n", p=128)
        wg_count = [6]

        def load_group(g):
            i = wg_count[0]
            wg_count[0] += 1
            wg = WG[i % 2]
            P.dma(POOL, wg[:, :, 0:GW[g]], win_v[:, :, GOFF[g]:GOFF[g] + GW[g]], w=["WG%d" % (i % 2)])
            return wg, "WG%d" % (i % 2)

        def tile_src(t):
            if t < 16:
                return x_in[t * 128:(t + 1) * 128, :], 0
            return xc_in[(t - 16) * 128:(t - 15) * 128, :], 1

        def step1(t):
            src, v = tile_src(t)
            xt = XT[t % 2]
            xn = xt
            xk = "XT%d" % (t % 2)
            nk = xk
            col = t % 32
            P.dma(SP, xt[:], src, w=[xk])
            P.op(ACT, lambda e: e.activation(
                out=JUNK[:], in_=xt[:], func=AF.Square, accum_out=SMALL[:, col:col + 1]),
                r=[xk], w=["JUNK", "SM%d" % col])
            P.op(DVE, lambda e: e.tensor_scalar(
                out=SMALL[:, col:col + 1], in0=SMALL[:, col:col + 1], scalar1=1.0 / D, scalar2=EPS,
                op0=ALU.mult, op1=ALU.add), w=["SM%d" % col])
            P.op(POOL, lambda e: e.tensor_tensor(
                out=SMALL[:, col:col + 1], in0=SMALL[:, col:col + 1], in1=NHALF[:, 0:1], op=ALU.pow),
                r=["NHALF"], w=["SM%d" % col])
            P.op(DVE, lambda e: e.tensor_scalar(
                out=xn[:], in0=xt[:], scalar1=SMALL[:, col:col + 1], scalar2=None, op0=ALU.mult),
                r=["SM%d" % col], w=[nk])
            for half in range(2):
                bank = half
                for kk in range(4):
                    k = half * 4 + kk
                    P.op(PE, lambda e, k=k, kk=kk, bank=bank: e.transpose(
                        out=PS[:, bank, kk * 128:(kk + 1) * 128], in_=xn[:, k * 128:(k + 1) * 128], identity=IDF[:]),
                        r=[nk, "IDF"], w=[psb(bank)])
                for kk in range(4):
                    k = half * 4 + kk
                    P.op(ACT, lambda e, k=k, kk=kk, bank=bank: e.activation(
                        out=HT[:, k, t * 128:(t + 1) * 128], in_=PS[:, bank, kk * 128:(kk + 1) * 128],
                        func=AF.Identity, scale=G1[:, k, v:v + 1], bias=MODT[:, k, v:v + 1]),
                        r=["G1", "MODT"], w=["HT%d" % t, psb(bank)])

        def qk_epilogue(ps_ap, nh, gain_off, gains_off, t, out_ap, key_out, bank):
            w = nh * 64
            col = 32 + (t % 2) * 8
            ss = SMALL[:, col:col + nh]
            ea, eb, ec = EA[:, 0:w], EB[:, 0:w], EC[:, 0:w]
            v3 = lambda a: a.rearrange("p (h d) -> p h d", d=64)
            P.op(ACT, lambda e: e.activation(out=ea, in_=ps_ap[:, 0:w], func=AF.Square), w=["EA", psb(bank)])
            P.op(DVE, lambda e: e.tensor_reduce(out=ss, in_=v3(ea), axis=AX.X, op=ALU.add), r=["EA"], w=["SS%d" % col])
            P.op(DVE, lambda e: e.tensor_scalar(out=ss, in0=ss, scalar1=1.0 / 64, scalar2=EPS, op0=ALU.mult, op1=ALU.add),
                 w=["SS%d" % col])
            P.op(POOL, lambda e: e.tensor_tensor(out=ss, in0=ss, in1=NHALF[:, 0:nh], op=ALU.pow),
                 r=["NHALF"], w=["SS%d" % col])
            ssb = ss.unsqueeze(2).to_broadcast([128, nh, 64])
            cosb = ROPE[:, 0, t, :].unsqueeze(1).to_broadcast([128, nh, 64])
            sinb = ROPE[:, 1, t, :].unsqueeze(1).to_broadcast([128, nh, 64])
            P.op(DVE, lambda e: e.tensor_tensor(out=v3(ea), in0=v3(ps_ap[:, 0:w]), in1=ssb, op=ALU.mult),
                 r=["SS%d" % col], w=["EA", psb(bank)])
            P.op(DVE, lambda e: e.tensor_tensor(out=ea, in0=ea, in1=SC[:, gain_off:gain_off + w], op=ALU.mult),
                 r=["SC"], w=["EA"])
            P.op(DVE, lambda e: e.tensor_tensor(out=v3(eb), in0=v3(ps_ap[:, w:2 * w]), in1=ssb, op=ALU.mult),
                 r=["SS%d" % col], w=["EB", psb(bank)])
            P.op(DVE, lambda e: e.tensor_tensor(out=eb, in0=eb, in1=SC[:, gains_off:gains_off + w], op=ALU.mult),
                 r=["SC"], w=["EB"])
            P.op(DVE, lambda e: e.tensor_tensor(out=v3(ea), in0=v3(ea), in1=cosb, op=ALU.mult), r=["ROPE"], w=["EA"])
            P.op(DVE, lambda e: e.tensor_tensor(out=v3(eb), in0=v3(eb), in1=sinb, op=ALU.mult), r=["ROPE"], w=["EB"])
            P.op(DVE, lambda e: e.tensor_tensor(out=out_ap, in0=ea, in1=eb, op=ALU.add), r=["EA", "EB"], w=[key_out])

        TB_TOK = [(0, 512), (512, 512), (1024, 512), (1536, 512), (2048, 256)]
        PSB16 = [PS[:, b, :].bitcast(BF16) for b in range(8)]

        wg, wk = load_group(0)
        for blk in range(5):
            tiles = range(4 * blk, min(4 * blk + 4, NT))
            for t in tiles:
                step1(t)
            for t in tiles:
                bank = 2 + (t % 2)
                for k in range(8):
                    P.op(PE, lambda e, k=k, t=t, bank=bank, wg=wg: e.matmul(
                        PS[:, bank, :], lhsT=HT[:, k, t * 128:(t + 1) * 128], rhs=wg[:, k, 0:512],
                        start=(k == 0), stop=(k == 7)), r=[wk, "HT%d" % t], w=[psb(bank)])
                qr = QR[t % 2]
                qk_epilogue(PS[:, bank, :], 4, SC_QG, SC_QGS, t, qr[:], "QR%d" % (t % 2), bank)
                tb_ = 4 + (t % 2)
                for gi in range(2):
                    P.op(PE, lambda e, gi=gi, tb_=tb_, qr=qr: e.transpose(
                        out=PSB16[tb_][:, gi * 128:(gi + 1) * 128], in_=qr[:, gi * 128:(gi + 1) * 128],
                        identity=TA[:, TAB_ID:TAB_ID + 128]), r=["QR%d" % (t % 2), "TA"], w=[psb(tb_)])
                P.op(ACT, lambda e, tb_=tb_, t=t: e.activation(
                    out=QT[:, :, t * 128:(t + 1) * 128],
                    in_=PSB16[tb_][:, 0:256].rearrange("p (g n) -> p g n", n=128), func=AF.Copy),
                    w=["QT", psb(tb_)])

        wg, wk = load_group(1)
        for t in range(NT):
            bank = 2 + (t % 2)
            for k in range(8):
                P.op(PE, lambda e, k=k, t=t, bank=bank, wg=wg: e.matmul(
                    PS[:, bank, 0:384], lhsT=HT[:, k, t * 128:(t + 1) * 128], rhs=wg[:, k, 0:384],
                    start=(k == 0), stop=(k == 7)), r=[wk, "HT%d" % t], w=[psb(bank)])
            kr = KR[t % 2]
            qk_epilogue(PS[:, bank, 0:256], 2, SC_KG, SC_KGS, t, kr[:], "KR%d" % (t % 2), bank)
            P.op(ACT, lambda e, t=t, bank=bank: e.activation(out=VS[:, t, :], in_=PS[:, bank, 256:384], func=AF.Copy),
                 w=["VS", psb(bank)])
            tb_ = 4 + (t % 2)
            P.op(PE, lambda e, tb_=tb_, kr=kr: e.transpose(
                out=PSB16[tb_][:, 0:128], in_=kr[:], identity=TA[:, TAB_ID:TAB_ID + 128]),
                r=["KR%d" % (t % 2), "TA"], w=[psb(tb_)])
            P.op(ACT, lambda e, tb_=tb_, t=t: e.activation(
                out=KTS[:, t * 128:(t + 1) * 128], in_=PSB16[tb_][:, 0:128], func=AF.Copy), w=["KTS", psb(tb_)])
        finals = []
        if mode == "A":
            finals.append(P.dma(SP, pK, KTS[:, 0:TLOC], r=["KTS"]))
            finals.append(P.dma(SP, pV.rearrange("(t p) n -> p t n", p=128), VS[:, 0:16, :], r=["VS"]))
        else:
            P.dma(SP, cK, KTS[:, TLOC:NTOK], r=["KTS"], w=["cK"])
            P.dma(SP, cV.rearrange("(t p) n -> p t n", p=128), VS[:, 16:18, :], r=["VS"], w=["cV"])

        wg, wk = load_group(2)
        for t in range(NT):
            bank = 2 + (t % 2)
            for k in range(8):
                P.op(PE, lambda e, k=k, t=t, bank=bank, wg=wg: e.matmul(
                    PS[:, bank, 0:256], lhsT=HT[:, k, t * 128:(t + 1) * 128], rhs=wg[:, k, 0:256],
                    start=(k == 0), stop=(k == 7)), r=[wk, "HT%d" % t], w=[psb(bank)])
            P.op(ACT, lambda e, t=t, bank=bank: e.activation(out=UP[:, t, :], in_=PS[:, bank, 0:256], func=AF.Copy),
                 w=["UP", psb(bank)])
        if mode == "A":
            finals.append(P.dma(SP, pUP[0], UP[:, 0, :], r=["UP"]))
            finals.append(P.dma(SP, pUP[1], UP[:, 15, :], r=["UP"]))

        def fm_group(wg, wk, blk, ncc):
            t0, n = TB_TOK[blk]
            for cc in range(ncc):
                for k in range(8):
                    P.op(PE, lambda e, k=k, cc=cc, wg=wg: e.matmul(
                        PS[:, 4 + cc, 0:n], lhsT=wg[:, k, cc * 128:(cc + 1) * 128], rhs=HT[:, k, t0:t0 + n],
                        start=(k == 0), stop=(k == 7)),
                        r=[wk] + ["HT%d" % t for t in range(t0 // 128, (t0 + n) // 128)], w=[psb(4 + cc)])

        def tt_off(blk):
            t0, n = TB_TOK[blk]
            return (1 + t0) if blk < 4 else 2051

        for c in range(2):
            wg, wk = load_group(3 + c)
            for blk in range(5):
                t0, n = TB_TOK[blk]
                fm_group(wg, wk, blk, 4)
                ct1, ct2 = CT1[blk % 2], CT2[blk % 2]
                k1, k2 = "CT1%d" % (blk % 2), "CT2%d" % (blk % 2)
                P.op(ACT, lambda e, n=n, ct1=ct1: e.activation(out=ct1[:, 0:n], in_=PS[:, 4, 0:n], func=AF.Copy),
                     w=[k1, psb(4)])
                o = tt_off(blk)
                P.op(DVE, lambda e, n=n, ct1=ct1, o=o, c=c: e.tensor_tensor(
                    out=TT[:, c, o:o + n], in0=PS[:, 5, 0:n], in1=ct1[:, 0:n], op=ALU.mult),
                    r=[k1], w=["TT", psb(5)])
                P.op(ACT, lambda e, n=n, ct2=ct2: e.activation(out=ct2[:, 0:n], in_=PS[:, 7, 0:n], func=AF.Silu),
                     w=[k2, psb(7)])
                P.op(DVE, lambda e, n=n, ct2=ct2, c=c, t0=t0: e.tensor_tensor(
                    out=OT[:, 4 + c, t0:t0 + n], in0=PS[:, 6, 0:n], in1=ct2[:, 0:n], op=ALU.mult),
                    r=[k2], w=["OT%d" % (4 + c), psb(6)])
        if mode == "A":
            for c in range(2):
                P.op(DVE, lambda e, c=c: e.tensor_copy(out=EDG[:, 2 * c:2 * c + 1], in_=TT[:, c, 1:2]), r=["TT"], w=["EDG"])
                P.op(DVE, lambda e, c=c: e.tensor_copy(out=EDG[:, 2 * c + 1:2 * c + 2], in_=TT[:, c, 2048:2049]),
                     r=["TT"], w=["EDG"])
            finals.append(P.dma(SP, pT, EDG[:], r=["EDG"]))

        wg, wk = load_group(5)
        for blk in range(5):
            t0, n = TB_TOK[blk]
            fm_group(wg, wk, blk, 4)
            for cc in range(4):
                dst = (2 + cc) if cc < 2 else (cc - 2)
                P.op(ACT, lambda e, n=n, cc=cc, dst=dst, t0=t0: e.activation(
                    out=OT[:, dst, t0:t0 + n], in_=PS[:, 4 + cc, 0:n], func=AF.Silu),
                    w=["OT%d" % dst, psb(4 + cc)])

        wg, wk = load_group(6)
        for blk in range(5):
            t0, n = TB_TOK[blk]
            fm_group(wg, wk, blk, 4)
            for cc in range(2):
                P.op(ACT, lambda e, n=n, cc=cc: e.activation(out=UFT[:, cc, 0:n], in_=PS[:, 4 + cc, 0:n], func=AF.Copy),
                     w=["UFT", psb(4 + cc)])
            for cc in range(2):
                P.op(ACT, lambda e, n=n, cc=cc, t0=t0: e.activation(
                    out=OT[:, 6 + cc, t0:t0 + n], in_=PS[:, 6 + cc, 0:n], func=AF.Silu),
                    w=["OT%d" % (6 + cc), psb(6 + cc)])
            for ti in range(n // 128):
                t = t0 // 128 + ti
                bank = 2 + (t % 2)
                for cc in range(2):
                    P.op(PE, lambda e, cc=cc, ti=ti, bank=bank: e.matmul(
                        PS[:, bank, :], lhsT=UFT[:, cc, ti * 128:(ti + 1) * 128],
                        rhs=TA[:, TAB_CCS + cc * 512:TAB_CCS + (cc + 1) * 512], start=(cc == 0), stop=(cc == 1)),
                        r=["UFT", "TA"], w=[psb(bank)])
                if t < 16:
                    ucs = UCS[t % 2]
                    uk = "UCS%d" % (t % 2)
                    P.op(DVE, lambda e, bank=bank, ucs=ucs: e.tensor_copy(out=ucs[:], in_=PS[:, bank, :]),
                         w=[uk, psb(bank)])
                    if mode == "A":
                        for reim in range(2):
                            finals.append(P.dma(
                                SP, pF[:, reim, t * 128:(t + 1) * 128, :].rearrange("q p c -> p q c"),
                                ucs[:, reim * 256:(reim + 1) * 256].rearrange("p (q c) -> p q c", c=64), r=[uk]))
                else:
                    P.op(DVE, lambda e, bank=bank, t=t: e.tensor_copy(out=UCSC[:, t - 16, :], in_=PS[:, bank, :]),
                         w=["UCSC", psb(bank)])

        if mode == "A":
            P.emit(final_wait_ops=finals)
            return nc

        P.barrier(SMALL[:, 63:64])
        arena_reset()
        TB = av([128, TB_END], BF16)
        WO = av([128, 8, 1024], BF16)
        keep_off = cursor[0]
        XG = av([128, 128, 64], BF16)
        BQ = av([128, 64, 128], BF16)
        YT = av([128, 2, NTOK], BF16)
        EUP = av([128, 8, 256], BF16)
        HUP = av([128, 2, 256], BF16)
        ET = av([128, 4, 4], BF16)
        ETF = av([128, 4, 4], F32)
        HTF = av([128, 4], F32)
        CA = [av([128, 512], F32) for _ in range(2)]
        DT = [av([64, 4, 128], BF16) for _ in range(2)]
        P.dma(SP, TB[:], tb_in, w=["TB"])
        P.dma(POOL, WO[:], wo_in.rearrange("(k p) n -> p k n", p=128), w=["WO"])
        P.dma(SP, EUP[:].rearrange("p (r s) c -> p r s c", s=2), gUP.rearrange("r s p c -> p r s c"), w=["EUP"])
        P.dma(SP, ET[:], gT.rearrange("r p f -> p r f"), w=["ET"])
        P.op(DVE, lambda e: e.tensor_copy(out=ETF[:], in_=ET[:]), r=["ET"], w=["ETF"])
        for c in range(2):
            for side in range(2):
                srccol = 2 * c + 1 if side == 0 else 2 * c
                sel = SC_SELL if side == 0 else SC_SELR
                hcol = 2 * c + side
                P.op(DVE, lambda e, srccol=srccol, sel=sel, hcol=hcol: e.tensor_scalar(
                    out=HTF[:, hcol:hcol + 1], in0=ETF[:, 0, srccol:srccol + 1], scalar1=SC[:, sel:sel + 1],
                    scalar2=None, op0=ALU.mult), r=["ETF", "SC"], w=["HTF"])
                for r_ in range(1, 4):
                    P.op(DVE, lambda e, srccol=srccol, sel=sel, hcol=hcol, r_=r_: e.scalar_tensor_tensor(
                        out=HTF[:, hcol:hcol + 1], in0=ETF[:, r_, srccol:srccol + 1],
                        scalar=SC[:, sel + r_:sel + r_ + 1], in1=HTF[:, hcol:hcol + 1], op0=ALU.mult, op1=ALU.add),
                        r=["ETF", "SC"], w=["HTF"])
                dstcol = 0 if side == 0 else 2049
                P.op(DVE, lambda e, c=c, hcol=hcol, dstcol=dstcol: e.tensor_copy(
                    out=TT[:, c, dstcol:dstcol + 1], in_=HTF[:, hcol:hcol + 1]), r=["HTF"], w=["TT"])
        for side in range(2):
            sel = SC_SELL if side == 0 else SC_SELR
            s_idx = 1 if side == 0 else 0
            P.op(DVE, lambda e, side=side, sel=sel, s_idx=s_idx: e.tensor_scalar(
                out=HUP[:, side, :], in0=EUP[:, s_idx, :], scalar1=SC[:, sel:sel + 1], scalar2=None, op0=ALU.mult),
                r=["EUP", "SC"], w=["HUP"])
            for r_ in range(1, 4):
                P.op(DVE, lambda e, side=side, sel=sel, s_idx=s_idx, r_=r_: e.scalar_tensor_tensor(
                    out=HUP[:, side, :], in0=EUP[:, 2 * r_ + s_idx, :], scalar=SC[:, sel + r_:sel + r_ + 1],
                    in1=HUP[:, side, :], op0=ALU.mult, op1=ALU.add), r=["EUP", "SC"], w=["HUP"])

        for c in range(2):
            for blk in range(5):
                t0, n = TB_TOK[blk]
                o = tt_off(blk)
                ca = CA[blk % 2]
                ck = "CA%d" % (blk % 2)
                cwl = [SC[:, SC_CW + c * 3 + tap:SC_CW + c * 3 + tap + 1] for tap in range(3)]
                cw = lambda tap, cwl=cwl: cwl[tap]
                P.op(DVE, lambda e, n=n, o=o, ca=ca, c=c, cw=cw: e.tensor_scalar(
                    out=ca[:, 0:n], in0=TT[:, c, o - 1:o - 1 + n], scalar1=cw(0), scalar2=None, op0=ALU.mult),
                    r=["TT", "SC"], w=[ck])
                for tap in (1, 2):
                    P.op(DVE, lambda e, n=n, o=o, ca=ca, c=c, cw=cw, tap=tap: e.scalar_tensor_tensor(
                        out=ca[:, 0:n], in0=TT[:, c, o - 1 + tap:o - 1 + tap + n], scalar=cw(tap), in1=ca[:, 0:n],
                        op0=ALU.mult, op1=ALU.add), r=["TT", "SC"], w=[ck])
                P.op(DVE, lambda e, n=n, ca=ca, c=c, t0=t0: e.scalar_tensor_tensor(
                    out=OT[:, 4 + c, t0:t0 + n], in0=ca[:, 0:n], scalar=SC[:, SC_CB + c:SC_CB + c + 1],
                    in1=OT[:, 4 + c, t0:t0 + n], op0=ALU.add, op1=ALU.mult), r=[ck, "SC"], w=["OT%d" % (4 + c)])

        def ptab(wi, kind):
            o = TB_POOL + (wi * 7 + kind) * 128
            return TB[:, o:o + 128]

        for t in range(NT):
            bank = 2 + (t % 2)
            if t < 16:
                prev = HUP[:, 0, :] if t == 0 else UP[:, t - 1, :]
                nxt = HUP[:, 1, :] if t == 15 else UP[:, t + 1, :]
                ckind = 3 if t == 0 else (4 if t == 15 else 2)
            else:
                prev = UP[:, 16, :] if t == 17 else None
                nxt = UP[:, 17, :] if t == 16 else None
                ckind = 5 if t == 16 else 6
            for g in range(4):
                srcs = [(UP[:, t, :], ckind)]
                if prev is not None:
                    srcs.append((prev, 0))
                if nxt is not None:
                    srcs.append((nxt, 1))
                for si, (src, kind) in enumerate(srcs):
                    P.op(PE, lambda e, g=g, src=src, kind=kind, si=si, ns=len(srcs), bank=bank: e.matmul(
                        PS[0:64, bank, g * 128:(g + 1) * 128], lhsT=src[:, g * 64:(g + 1) * 64], rhs=ptab(g, kind),
                        start=(si == 0), stop=(si == ns - 1)), r=["UP", "HUP", "TB"], w=[psb(bank)])
            dt_ = DT[t % 2]
            dk = "DT%d" % (t % 2)
            P.op(ACT, lambda e, dt_=dt_, bank=bank: e.activation(
                out=dt_[:], in_=PS[0:64, bank, :].rearrange("p (g n) -> p g n", n=128), func=AF.Copy),
                w=[dk, psb(bank)])
            b2 = 4 + (t % 2)
            for c in range(2):
                for gg in range(2):
                    g = 2 * c + gg
                    P.op(PE, lambda e, c=c, g=g, gg=gg, dt_=dt_, b2=b2: e.matmul(
                        PS[:, b2, c * 128:(c + 1) * 128], lhsT=PWP[:, g, :], rhs=dt_[:, g, :],
                        start=(gg == 0), stop=(gg == 1)), r=[dk, "PWP"], w=[psb(b2)])
            for c in range(2):
                P.op(DVE, lambda e, c=c, t=t, b2=b2: e.scalar_tensor_tensor(
                    out=OT[:, 6 + c, t * 128:(t + 1) * 128], in0=PS[:, b2, c * 128:(c + 1) * 128],
                    scalar=SC[:, SC_PS + c:SC_PS + c + 1], in1=OT[:, 6 + c, t * 128:(t + 1) * 128],
                    op0=ALU.mult, op1=ALU.mult), r=["SC"], w=["OT%d" % (6 + c), psb(b2)])

        for q in range(4):
            for r_ in range(4):
                for reim in range(2):
                    P.dma(SP, XG[reim * 64 + r_ * 16:reim * 64 + r_ * 16 + 16, :, :],
                          gF[r_, q, reim].rearrange("(i n) c -> i n c", n=128), w=["XG%d" % (reim * 4 + r_)])
            for c4 in range(16):
                bank = c4 % 2
                for ci in range(4):
                    ch = c4 * 4 + ci
                    P.op(PE, lambda e, ch=ch, ci=ci, bank=bank: e.matmul(
                        PS[:, bank, ci * 128:(ci + 1) * 128], lhsT=XG[:, :, ch], rhs=TA[:, TAB_W1:TAB_W1 + 128],
                        start=True, stop=True), r=["XG%d" % i for i in range(8)] + ["TA"], w=[psb(bank)])
                eng = ACT if c4 % 2 == 0 else DVE
                if eng == ACT:
                    P.op(ACT, lambda e, c4=c4, bank=bank: e.activation(
                        out=BQ[:, c4 * 4:(c4 + 1) * 4, :], in_=PS[:, bank, :].rearrange("p (a b) -> p a b", b=128),
                        func=AF.Copy), w=["BQ", psb(bank)])
                else:
                    P.op(DVE, lambda e, c4=c4, bank=bank: e.tensor_copy(
                        out=BQ[:, c4 * 4:(c4 + 1) * 4, :], in_=PS[:, bank, :].rearrange("p (a b) -> p a b", b=128)),
                        w=["BQ", psb(bank)])
            for k2 in range(64):
                bank = 4 + k2 // 16
                oc = (k2 % 16) * 32
                P.op(PE, lambda e, k2=k2, bank=bank, oc=oc: e.matmul(
                    PS[0:64, bank, oc:oc + 32], lhsT=BQ[:, :, k2], rhs=TB[:, TB_T2C + k2 * 32:TB_T2C + (k2 + 1) * 32],
                    start=True, stop=False), r=["BQ", "TB"], w=[psb(bank)])
                P.op(PE, lambda e, k2=k2, bank=bank, oc=oc: e.matmul(
                    PS[0:64, bank, oc:oc + 32], lhsT=BQ[:, :, 64 + k2], rhs=TB[:, TB_T2S + k2 * 32:TB_T2S + (k2 + 1) * 32],
                    start=False, stop=True), r=["BQ", "TB"], w=[psb(bank)])
            for bi in range(4):
                po = (q % 2) * 64
                dst = YT[po:po + 64, q // 2, 0:TLOC].rearrange("p (k1 k2) -> p k2 k1", k2=64)[:, bi * 16:(bi + 1) * 16, :]
                src = PS[0:64, 4 + bi, :].rearrange("p (k2 k1) -> p k2 k1", k1=32)
                if bi % 2 == 0:
                    P.op(ACT, lambda e, dst=dst, src=src: e.activation(out=dst, in_=src, func=AF.Copy),
                         w=["YT", psb(4 + bi)])
                else:
                    P.op(DVE, lambda e, dst=dst, src=src: e.tensor_copy(out=dst, in_=src), w=["YT", psb(4 + bi)])
        for c in range(2):
            idx = 0
            for tl in range(2):
                for reim in range(2):
                    o = TB_TCC + (tl * 2 + reim) * 256
                    P.op(PE, lambda e, c=c, tl=tl, reim=reim, o=o, idx=idx: e.matmul(
                        PS[:, 2 + c, 0:256], lhsT=UCSC[:, tl, reim * 256 + c * 128:reim * 256 + (c + 1) * 128],
                        rhs=TB[:, o:o + 256], start=(idx == 0), stop=(idx == 3)), r=["UCSC", "TB"], w=[psb(2 + c)])
                    idx += 1
            P.op(ACT, lambda e, c=c: e.activation(out=YT[:, c, TLOC:NTOK], in_=PS[:, 2 + c, 0:256], func=AF.Copy),
                 w=["YT", psb(2 + c)])
        for blk in range(5):
            t0, n = TB_TOK[blk]
            for oc in range(2):
                bank = 2 * (blk % 2) + oc
                for ic in range(2):
                    P.op(PE, lambda e, oc=oc, ic=ic, bank=bank, t0=t0, n=n: e.matmul(
                        PS[:, bank, 0:n], lhsT=WF[:, ic, oc * 128:(oc + 1) * 128], rhs=YT[:, ic, t0:t0 + n],
                        start=(ic == 0), stop=(ic == 1)), r=["WF", "YT"], w=[psb(bank)])
                P.op(DVE, lambda e, oc=oc, bank=bank, t0=t0, n=n: e.tensor_tensor(
                    out=OT[:, oc, t0:t0 + n], in0=PS[:, bank, 0:n], in1=OT[:, oc, t0:t0 + n], op=ALU.mult),
                    w=["OT%d" % oc, psb(bank)])

        P.barrier(SMALL[:, 63:64])
        cursor[0] = keep_off
        KT = av([128, 66 * 128], BF16)
        VA = av([128, 66, 192], BF16)
        PT = [av([128, 512], BF16) for _ in range(4)]
        RC = av([128, 512], F32)
        RC2 = av([128, 512], F32)
        TMP = av([128, 512], F32)
        P.dma(SP, KT[:, 0:NCTX], cK, r=["cK"], w=["KT"])
        for r_ in range(4):
            P.dma(SP, KT[:, NCTX + r_ * TLOC:NCTX + (r_ + 1) * TLOC], gK[r_], w=["KT"])
        P.op(DVE, lambda e: e.memset(VA[:, :, 64:128], 1.0), w=["VA"])
        for kv in range(2):
            P.dma(SP, VA[:, 0:2, kv * 128:kv * 128 + 64],
                  cV.rearrange("(t p) n -> p t n", p=128)[:, :, kv * 64:(kv + 1) * 64], r=["cV"], w=["VA"])
            for r_ in range(4):
                P.dma(SP, VA[:, 2 + 16 * r_:2 + 16 * (r_ + 1), kv * 128:kv * 128 + 64],
                      gV[r_].rearrange("(t p) n -> p t n", p=128)[:, :, kv * 64:(kv + 1) * 64], w=["VA"])

        it = 0
        for blk in range(5):
            t0, n = TB_TOK[blk]
            chunks = list(range(66)) if blk < 4 else [0, 1]
            for gi in range(2):
                for s in range(2):
                    ob = 4 + (it % 2)
                    it += 1
                    rs = 64 * (1 - s)
                    os_ = 64 * s
                    nch = len(chunks)
                    for ci, ch in enumerate(chunks):
                        sbk = ci % 4
                        P.op(PE, lambda e, s=s, gi=gi, ch=ch, sbk=sbk, t0=t0, n=n: e.matmul(
                            PS[:, sbk, 0:n], lhsT=KT[64 * s:64 * s + 64, ch * 128:(ch + 1) * 128],
                            rhs=QT[64 * s:64 * s + 64, gi, t0:t0 + n], start=True, stop=True),
                            r=["KT", "QT"], w=[psb(sbk)])
                        pt = PT[ci % 4]
                        pk = "PT%d" % (ci % 4)
                        P.op(ACT, lambda e, sbk=sbk, pt=pt, n=n: e.activation(
                            out=pt[:, 0:n], in_=PS[:, sbk, 0:n], func=AF.Exp, scale=0.125), w=[pk, psb(sbk)])
                        P.op(PE, lambda e, s=s, ch=ch, pt=pt, ob=ob, ci=ci, nch=nch, n=n: e.matmul(
                            PS[:, ob, 0:n], lhsT=VA[:, ch, 64 * s:64 * s + 128], rhs=pt[:, 0:n],
                            start=(ci == 0), stop=(ci == nch - 1)), r=["VA", pk], w=[psb(ob)])
                    P.op(DVE, lambda e, rs=rs, ob=ob, n=n: e.reciprocal(out=RC[rs:rs + 64, 0:n], in_=PS[rs:rs + 64, ob, 0:n]),
                         w=["RC", psb(ob)])
                    P.op(DVE, lambda e, rs=rs, os_=os_, n=n: e.tensor_copy(out=RC2[os_:os_ + 64, 0:n], in_=RC[rs:rs + 64, 0:n]),
                         r=["RC"], w=["RC2"])
                    P.op(DVE, lambda e, os_=os_, ob=ob, n=n: e.tensor_tensor(
                        out=TMP[os_:os_ + 64, 0:n], in0=PS[os_:os_ + 64, ob, 0:n], in1=RC2[os_:os_ + 64, 0:n], op=ALU.mult),
                        r=["RC2"], w=["TMP", psb(ob)])
                    P.op(DVE, lambda e, os_=os_, gi=gi, t0=t0, n=n: e.tensor_tensor(
                        out=OT[os_:os_ + 64, 2 + gi, t0:t0 + n], in0=TMP[os_:os_ + 64, 0:n],
                        in1=OT[os_:os_ + 64, 2 + gi, t0:t0 + n], op=ALU.mult), r=["TMP"], w=["OT%d" % (2 + gi)])

        P.barrier(SMALL[:, 63:64])
        cursor[0] = keep_off
        XR = [av([128, 1024], F32) for _ in range(2)]
        XO = [av([128, 1024], F32) for _ in range(2)]
        for t in range(NT):
            src, v = tile_src(t)
            xr, xo = XR[t % 2], XO[t % 2]
            rk, ok = "XR%d" % (t % 2), "XO%d" % (t % 2)
            P.dma(SP, xr[:], src, w=[rk])
            for half in range(2):
                bank = (2 * t + half) % 4
                for k in range(8):
                    P.op(PE, lambda e, k=k, t=t, half=half, bank=bank: e.matmul(
                        PS[:, bank, :], lhsT=OT[:, k, t * 128:(t + 1) * 128], rhs=WO[:, k, half * 512:(half + 1) * 512],
                        start=(k == 0), stop=(k == 7)), r=["OT%d" % k, "WO"], w=[psb(bank)])
                P.op(DVE, lambda e, half=half, bank=bank, v=v, xo=xo: e.tensor_tensor(
                    out=xo[:, half * 512:(half + 1) * 512], in0=PS[:, bank, :], in1=GATE[:, v, half * 512:(half + 1) * 512],
                    op=ALU.mult), r=["GATE"], w=[ok, psb(bank)])
                P.op(DVE, lambda e, half=half, xo=xo, xr=xr: e.tensor_tensor(
                    out=xo[:, half * 512:(half + 1) * 512], in0=xo[:, half * 512:(half + 1) * 512],
                    in1=xr[:, half * 512:(half + 1) * 512], op=ALU.add), r=[rk], w=[ok])
            dst = x_out[t * 128:(t + 1) * 128, :] if t < 16 else xc_out[(t - 16) * 128:(t - 15) * 128, :]
            finals.append(P.dma(SP, dst, xo[:], r=[ok]))
        P.emit(final_wait_ops=finals)
    return nc


_PROGS = {}


def _prog(mode):
    if mode not in _PROGS:
        _PROGS[mode] = build_program(mode)
    return _PROGS[mode]


def _run_layer(l, x, xc, inp):
    idf = np.eye(128, dtype=np.float32)
    base = []
    for core in range(8):
        b, j = core // 4, core % 4
        ta, tb, rope = _tables(j)
        m = _layer_host_inputs(l, b, j, inp)
        m.update({
            "x": np.ascontiguousarray(x[b, j * TLOC:(j + 1) * TLOC]), "xc": np.ascontiguousarray(xc[b]),
            "tab_a": ta, "tab_b": tb, "rope": rope, "idf": idf,
        })
        base.append(m)
    resA = run_bass_kernel_spmd(_prog("A"), base, core_ids=list(range(8))).results
    maps = []
    for core in range(8):
        b = core // 4
        grp = [resA[b * 4 + r] for r in range(4)]
        m = dict(base[core])
        m["gK"] = np.stack([g["pK"] for g in grp])
        m["gV"] = np.stack([g["pV"] for g in grp])
        m["gF"] = np.stack([g["pF"] for g in grp])
        m["gUP"] = np.stack([g["pUP"] for g in grp])
        m["gT"] = np.stack([g["pT"] for g in grp])
        maps.append(m)
    resB = run_bass_kernel_spmd(_prog("AB"), maps, core_ids=list(range(8))).results
    xn = np.stack([np.concatenate([resB[b * 4 + j]["x_out"] for j in range(4)], axis=0) for b in range(2)])
    xcn = np.stack([resB[b * 4]["xc_out"] for b in range(2)])
    return xn, xcn


def kernel(**inputs):
    inp = {k: np.asarray(v) for k, v in inputs.items()}
    x = np.asarray(inp["x"], np.float32)
    xc = np.asarray(inp["ctx"], np.float32)
    for l in range(DEPTH):
        x, xc = _run_layer(l, x, xc, inp)
    return x.astype(np.float32)
```

```python
import contextlib
import os
import numpy as np
import ml_dtypes
import concourse.bass as bass
import concourse.mybir as mybir
from concourse.bass_utils import run_bass_kernel_spmd

F32 = mybir.dt.float32
BF16 = mybir.dt.bfloat16
U8 = mybir.dt.uint8
AF = mybir.ActivationFunctionType
ALU = mybir.AluOpType
AX = mybir.AxisListType

PE, ACT, DVE, POOL, SP = "tensor", "scalar", "vector", "gpsimd", "sync"
ENGS = [PE, ACT, DVE, POOL, SP]

D = 1024
SEQ = 8192
NCTX = 256
DEPTH = 2
TLOC = 2048
NT = 18
NTOK = NT * 128
EPS = 1e-6
HO = [0, 2, 1, 3]
GW = [512, 384, 256, 512, 512, 512, 512]
GOFF = [0, 512, 896, 1152, 1664, 2176, 2688]
DIN2 = 3200
_NO_CC = bool(os.environ.get("KNOCC"))


class Op:
    __slots__ = ("eng", "fn", "deps", "dma", "idx", "sig", "sigval", "dsem", "dval", "prev_same_sem", "cc")

    def __init__(self, eng, fn, dma):
        self.eng = eng
        self.fn = fn
        self.dma = dma
        self.cc = False
        self.deps = set()
        self.sig = False
        self.sigval = None
        self.dsem = None
        self.dval = None
        self.prev_same_sem = None


class Prog:
    def __init__(self, nc, n_dma_sems=16, same_engine_sync=True):
        self.nc = nc
        self.ops = []
        self.per_eng = {e: [] for e in ENGS}
        self.last_w = {}
        self.readers = {}
        self.same_engine_sync = same_engine_sync
        self.n_dma_sems = n_dma_sems
        self.barrier_op = None
        self.dmas_since_barrier = []

    def op(self, eng, fn, r=(), w=(), dma=False):
        o = Op(eng, fn, dma)
        o.idx = len(self.ops)
        self.ops.append(o)
        self.per_eng[eng].append(o)
        for k in r:
            lw = self.last_w.get(k)
            if lw is not None:
                o.deps.add(lw)
        for k in w:
            lw = self.last_w.get(k)
            if lw is not None:
                o.deps.add(lw)
            for rd in self.readers.get(k, ()):
                o.deps.add(rd)
        for k in r:
            self.readers.setdefault(k, []).append(o.idx)
        for k in w:
            self.last_w[k] = o.idx
            self.readers[k] = []
        if self.barrier_op is not None:
            o.deps.add(self.barrier_op)
        o.deps.discard(o.idx)
        if dma:
            self.dmas_since_barrier.append(o.idx)
        return o

    def dma(self, eng, out, in_, r=(), w=(), **kw):
        return self.op(eng, lambda e: e.dma_start(out=out, in_=in_, **kw), r=r, w=w, dma=True)

    def cc(self, fn, r=(), w=()):
        o = self.op(POOL, fn, r=r, w=w, dma=True)
        o.cc = True
        return o

    def barrier(self, scratch):
        o = self.op(DVE, lambda e: e.memset(scratch, 0.0))
        for e in ENGS:
            if self.per_eng[e]:
                lst = [x for x in self.per_eng[e] if x.idx != o.idx]
                if lst:
                    o.deps.add(lst[-1].idx)
        for d in self.dmas_since_barrier:
            o.deps.add(d)
        self.dmas_since_barrier = []
        self.barrier_op = o.idx
        self.last_w = {}
        self.readers = {}
        return o

    def check(self):
        done = [False] * len(self.ops)
        pos = {e: 0 for e in ENGS}
        progress = True
        while progress:
            progress = False
            for e in ENGS:
                lst = self.per_eng[e]
                while pos[e] < len(lst):
                    o = lst[pos[e]]
                    if all(done[d] for d in o.deps):
                        done[o.idx] = True
                        pos[e] += 1
                        progress = True
                    else:
                        break
        stuck = {e: (pos[e], len(self.per_eng[e])) for e in ENGS if pos[e] < len(self.per_eng[e])}
        return stuck

    def emit(self, final_wait_ops=()):
        nc = self.nc
        ops = self.ops
        stuck = self.check()
        assert not stuck, ("deadlock in recorded program", stuck)
        for o in ops:
            nd = set()
            for d in o.deps:
                y = ops[d]
                if y.eng == o.eng and not y.dma and not o.dma:
                    if o.eng == PE or not self.same_engine_sync:
                        continue
                if y.eng == o.eng and not y.dma and o.dma and o.eng == SP:
                    continue
                nd.add(d)
            o.deps = nd
        for o in ops:
            for d in o.deps:
                ops[d].sig = True
        for d in final_wait_ops:
            d.sig = True
        stack = contextlib.ExitStack()
        esem = {}
        for e in ENGS:
            esem[e] = stack.enter_context(nc.semaphore("cs_" + e))
        dsems = {}
        for e in ENGS:
            dsems[e] = [stack.enter_context(nc.semaphore("ds_%s_%d" % (e, i))) for i in range(self.n_dma_sems)] \
                if any(o.dma for o in self.per_eng[e]) else []
        for e in ENGS:
            cnt = 0
            di = 0
            last_on_sem = {}
            semcnt = {}
            for o in self.per_eng[e]:
                if o.cc:
                    o.sig = True
                    o.dsem = stack.enter_context(nc.semaphore("ccs_%d" % o.idx))
                    o.dval = 1
                    o.prev_same_sem = None
                elif o.dma:
                    o.sig = True
                    si = di % self.n_dma_sems
                    di += 1
                    o.dsem = dsems[e][si]
                    semcnt[si] = semcnt.get(si, 0) + 16
                    o.dval = semcnt[si]
                    o.prev_same_sem = last_on_sem.get(si)
                    last_on_sem[si] = o
                elif o.sig:
                    cnt += 1
                    o.sigval = cnt
        block = stack.enter_context(nc.Block())

        def make_body(e):
            def body(eng):
                waited = {}

                def wait(sem, val, key):
                    if waited.get(key, 0) >= val:
                        return
                    eng.wait_ge(sem, val)
                    waited[key] = val

                def wait_op(y):
                    if y.dma:
                        wait(y.dsem, y.dval, ("d", y.eng, id(y.dsem)))
                    else:
                        wait(esem[y.eng], y.sigval, ("c", y.eng))

                for o in self.per_eng[e]:
                    for d in sorted(o.deps):
                        wait_op(ops[d])
                    if o.dma and o.prev_same_sem is not None:
                        wait_op(o.prev_same_sem)
                    ins = o.fn(eng)
                    if o.cc:
                        ins.then_inc(o.dsem)
                    elif o.dma:
                        ins.then_inc(o.dsem, 16)
                    elif o.sig:
                        ins.then_inc(esem[e], 1)
                if e == SP:
                    for o in final_wait_ops:
                        wait_op(o)
            return body

        for e in ENGS:
            if self.per_eng[e] or e == SP:
                getattr(block, e)(make_body(e))
        stack.close()


TAB_ID = 0
TAB_CCS = 128
TAB_W1 = TAB_CCS + 1024
TAB_A_END = TAB_W1 + 128
TB_T2C = 0
TB_T2S = 2048
TB_TCC = 4096
TB_POOL = 5120
TB_END = TB_POOL + 4 * 7 * 128
POOL_WINDOWS = (2, 4, 8, 16)


def _band(w, nseq, out_start, in_start):
    left = w // 2
    right = w - 1 - left
    m = np.zeros((128, 128), np.float64)
    for o in range(128):
        go = out_start + o
        if go < 0 or go >= nseq:
            continue
        lo = min(max(go - left, 0), nseq)
        hi = min(max(go + right + 1, 0), nseq)
        cnt = hi - lo
        for gi in range(lo, hi):
            i = gi - in_start
            if 0 <= i < 128:
                m[i, o] += 1.0 / cnt
        i = go - in_start
        if 0 <= i < 128:
            m[i, o] -= 1.0
    return m


_CONST_CACHE = {}


def _tables(j):
    if j in _CONST_CACHE:
        return _CONST_CACHE[j]
    ta = np.zeros((128, TAB_A_END), np.float64)
    ta[:, TAB_ID:TAB_ID + 128] = np.eye(128)
    cc = np.zeros((256, 512))
    c = np.arange(64)
    ang = 2 * np.pi * np.outer(c, c) / 64.0
    for h in range(4):
        cc[h * 64:(h + 1) * 64, h * 64:(h + 1) * 64] = np.cos(ang)
        cc[h * 64:(h + 1) * 64, 256 + h * 64:256 + (h + 1) * 64] = -np.sin(ang)
    ta[:, TAB_CCS:TAB_CCS + 1024] = cc.reshape(2, 128, 512).transpose(1, 0, 2).reshape(128, 1024)
    n2 = np.arange(64)
    a1 = 2 * np.pi * np.outer(n2, n2) / 64.0
    w1 = np.zeros((128, 128))
    w1[0:64, 0:64] = np.cos(a1)
    w1[64:128, 0:64] = np.sin(a1)
    w1[0:64, 64:128] = -np.sin(a1)
    w1[64:128, 64:128] = np.cos(a1)
    ta[:, TAB_W1:TAB_W1 + 128] = w1
    tb = np.zeros((128, TB_END), np.float64)
    n1 = np.arange(128)[:, None, None]
    k2 = np.arange(64)[None, :, None]
    k1 = (32 * j + np.arange(32))[None, None, :]
    kk = 64 * k1 + k2
    a2 = 2 * np.pi * ((kk * n1) % SEQ) / float(SEQ)
    sc = 1.0 / np.sqrt(SEQ * 64.0)
    tb[:, TB_T2C:TB_T2C + 2048] = (np.cos(a2) * sc).reshape(128, 2048)
    tb[:, TB_T2S:TB_T2S + 2048] = (np.sin(a2) * sc).reshape(128, 2048)
    n = (np.arange(2)[None, :, None] * 128 + np.arange(128)[:, None, None])
    k = np.arange(256)[None, None, :]
    ac = 2 * np.pi * ((n * k) % 256) / 256.0
    scc = 1.0 / np.sqrt(256 * 64.0)
    tcc = np.stack([np.cos(ac) * scc, np.sin(ac) * scc], axis=2)
    tb[:, TB_TCC:TB_TCC + 1024] = tcc.reshape(128, 1024)
    pt = np.zeros((128, 4, 7, 128))
    for wi, w in enumerate(POOL_WINDOWS):
        big = 1 << 20
        pt[:, wi, 0] = _band(w, big, 1280, 1152)
        pt[:, wi, 1] = _band(w, big, 1280, 1408)
        pt[:, wi, 2] = _band(w, big, 1280, 1280)
        pt[:, wi, 3] = _band(w, SEQ, 0, 0) if j == 0 else pt[:, wi, 2]
        pt[:, wi, 4] = _band(w, SEQ, SEQ - 128, SEQ - 128) if j == 3 else pt[:, wi, 2]
        pt[:, wi, 5] = _band(w, NCTX, 0, 0)
        pt[:, wi, 6] = _band(w, NCTX, 128, 128)
    tb[:, TB_POOL:TB_END] = pt.reshape(128, -1)
    inv = 10000.0 ** (-np.arange(16, dtype=np.float64) / 16.0)
    rope = np.zeros((128, 2, NT, 64), np.float64)
    rope[:, 0, :, :] = 1.0
    for t in range(16):
        tok = 2048 * j + t * 128 + np.arange(128)
        row = (tok // 64).astype(np.float64)[:, None] * inv
        col = (tok % 64).astype(np.float64)[:, None] * inv
        rope[:, 0, t, 0:16] = np.cos(row)
        rope[:, 0, t, 16:32] = np.cos(row)
        rope[:, 0, t, 32:48] = np.cos(col)
        rope[:, 0, t, 48:64] = np.cos(col)
        rope[:, 1, t, 0:16] = -np.sin(row)
        rope[:, 1, t, 16:32] = np.sin(row)
        rope[:, 1, t, 32:48] = -np.sin(col)
        rope[:, 1, t, 48:64] = np.sin(col)
    out = (ta.astype(ml_dtypes.bfloat16), tb.astype(ml_dtypes.bfloat16), rope.astype(np.float32))
    _CONST_CACHE[j] = out
    return out


def _swap_idx():
    d = np.arange(64)
    return np.where((d % 32) < 16, d + 16, d - 16)


def _win_perm():
    sw = _swap_idx()
    q0, k0, v0, za, uf, zf, bc, cc, hc, zc, up, zp = np.cumsum([0, 256, 128, 128, 256, 256, 256, 256, 256, 256, 256, 256])
    cols = []
    cols += [q0 + h * 64 + d for h in HO for d in range(64)]
    cols += [q0 + h * 64 + sw[d] for h in HO for d in range(64)]
    cols += [k0 + i for i in range(128)]
    cols += [k0 + kv * 64 + sw[d] for kv in range(2) for d in range(64)]
    cols += [v0 + i for i in range(128)]
    cols += [up + i for i in range(256)]
    for c in range(2):
        for base in (cc, hc, bc, zc):
            cols += [base + c * 128 + i for i in range(128)]
    cols += [za + h * 64 + d for h in HO for d in range(64)]
    cols += [zf + i for i in range(256)]
    cols += [uf + i for i in range(256)]
    cols += [zp + i for i in range(256)]
    assert len(cols) == DIN2
    return np.array(cols)


SC_C = 0
SC_BMOD = 16
SC_G = 40
SC_CW = 48
SC_CB = 54
SC_PS = 56
SC_SELL = 58
SC_SELR = 62
SC_QG = 66
SC_QGS = 322
SC_KG = 578
SC_KGS = 706
SC_END = 834


def _layer_host_inputs(l, b, j, inp):
    f32 = np.float32
    sw = _swap_idx()
    sc = np.zeros((128, SC_END), f32)
    cb = np.asarray(inp["c"][b], f32).reshape(8, 128).T
    cx = np.asarray(inp["c_ctx"], f32).reshape(8, 128).T
    sc[:, SC_C + 0:SC_C + 16:2] = cb
    sc[:, SC_C + 1:SC_C + 16:2] = cx
    sc[:, SC_BMOD:SC_BMOD + 24] = np.asarray(inp["b_mod"][l], f32).reshape(24, 128).T
    sc[:, SC_G:SC_G + 8] = np.asarray(inp["norm_g"][l], f32).reshape(8, 128).T
    cw = np.asarray(inp["conv_w"][l], f32)
    for c in range(2):
        for t in range(3):
            sc[:, SC_CW + c * 3 + t] = cw[t, c * 128:(c + 1) * 128]
    sc[:, SC_CB:SC_CB + 2] = np.asarray(inp["conv_b"][l], f32).reshape(2, 128).T
    sc[:, SC_PS:SC_PS + 2] = np.asarray(inp["pool_scale"][l], f32).reshape(2, 128).T
    if j > 0:
        sc[:, SC_SELL + j - 1] = 1.0
    if j < 3:
        sc[:, SC_SELR + j + 1] = 1.0
    qg = np.asarray(inp["q_gain"][l], f32)
    kg = np.asarray(inp["k_gain"][l], f32)
    sc[:, SC_QG:SC_QG + 256] = np.tile(qg, 4)[None, :]
    sc[:, SC_QGS:SC_QGS + 256] = np.tile(qg[sw], 4)[None, :]
    sc[:, SC_KG:SC_KG + 128] = np.tile(kg, 2)[None, :]
    sc[:, SC_KGS:SC_KGS + 128] = np.tile(kg[sw], 2)[None, :]
    bg = np.ascontiguousarray(np.broadcast_to(np.asarray(inp["b_mod"][l], f32)[2048:3072][None, :], (128, 1024)))
    w_in = np.ascontiguousarray(np.asarray(inp["w_in"][l], f32)[:, _win_perm()])
    pw = np.asarray(inp["pool_w"][l], f32)
    pwp = np.zeros((64, 4, 128), f32)
    for g in range(4):
        off = 64 * (g % 2)
        pwp[:, g, off:off + 64] = pw[g]
    w_out = np.asarray(inp["w_out"][l], f32)
    rows = np.arange(1024)
    rows[256:512] = np.array([256 + h * 64 + d for h in HO for d in range(64)])
    w_out = np.ascontiguousarray(w_out[rows])
    return {
        "sc": sc, "bg": bg, "w_mod": np.ascontiguousarray(np.asarray(inp["w_mod"][l], f32)),
        "w_in": w_in, "pwp": pwp, "w_four": np.ascontiguousarray(np.asarray(inp["w_fourier"][l], f32)),
        "w_out": w_out,
    }


ARENA_BYTES = 114 * 1024


def build_program(mode, dbg=False):
    nc = bass.Bass("TRN2", target_bir_lowering=False)

    def din(name, shape, dt=F32):
        return nc.dram_tensor(name, list(shape), dt, kind="ExternalInput").ap()

    def dout(name, shape, dt=F32):
        return nc.dram_tensor(name, list(shape), dt, kind="ExternalOutput").ap()

    NL = 1 if mode == "DBG1" else DEPTH
    x_in0 = din("x", [TLOC, D])
    xc_in0 = din("xc", [NCTX, D])
    sc_ins = [din("sc%d" % l, [128, SC_END]) for l in range(NL)]
    bg_ins = [din("bg%d" % l, [128, 1024]) for l in range(NL)]
    wmod_ins = [din("w_mod%d" % l, [D, 3 * D]) for l in range(NL)]
    win_ins = [din("w_in%d" % l, [D, DIN2]) for l in range(NL)]
    pwp_ins = [din("pwp%d" % l, [64, 4, 128]) for l in range(NL)]
    wf_ins = [din("w_four%d" % l, [256, 256]) for l in range(NL)]
    wo_ins = [din("w_out%d" % l, [D, D]) for l in range(NL)]
    ta_in = din("tab_a", [128, TAB_A_END], BF16)
    tb_in = din("tab_b", [128, TB_END], BF16)
    rope_in = din("rope", [128, 2, NT, 64])
    idf_in = din("idf", [128, 128])
    boff_in = din("boff", [1, 4], mybir.dt.int32)
    x_out = dout("x_out", [TLOC, D])
    PR = 3201
    DBG = (mode == "DBG1")
    if DBG:
        pub = dout("pub", [PR, 512], BF16)
        gath = din("gath", [8 * PR, 512], BF16)
        x1_d = dout("x1_d", [TLOC, D])
        xc1_d = dout("xc1_d", [NCTX, D])
    else:
        pub = nc.dram_tensor("pub", [PR, 512], BF16).ap()
        gath = nc.dram_tensor("gath", [8 * PR, 512], BF16).ap()
        x1_d = nc.dram_tensor("x1_d", [TLOC, D], F32).ap()
        xc1_d = nc.dram_tensor("xc1_d", [NCTX, D], F32).ap()
    gloc = nc.dram_tensor("gloc", [4, PR, 512], BF16).ap()
    gath3 = gath.rearrange("(r a) c -> r a c", a=PR)
    cK = nc.dram_tensor("cK", [128, NCTX], BF16).ap()
    cV = nc.dram_tensor("cV", [NCTX, 128], BF16).ap()
    pK = pub[0:512, :].rearrange("(p a) c -> p (a c)", p=128)
    pV = pub[512:1024, :].rearrange("r (a n) -> (r a) n", n=128)
    pF = pub[1024:3072, :].rearrange("(q e t) (a c) -> q e (t a) c", q=4, e=2, c=64)
    pUP = pub[3072:3200, :].rearrange("(s p2) (a c) -> s (p2 a) c", s=2, c=256)
    pT = pub[3200:3201, :].rearrange("o (p f) -> (o p) f", f=4)
    gK = [gloc[r, 0:512, :].rearrange("(p a) c -> p (a c)", p=128) for r in range(4)]
    gV = [gloc[r, 512:1024, :].rearrange("r (a n) -> (r a) n", n=128) for r in range(4)]
    gF = [gloc[r, 1024:3072, :].rearrange("(q e t) (a c) -> q e (t a) c", q=4, e=2, c=64) for r in range(4)]
    gUP = gloc[:, 3072:3200, :].rearrange("r (s p2) (a c) -> r s (p2 a) c", s=2, c=256)
    gT = gloc[:, 3200, :].rearrange("r (p f) -> r p f", f=4)
    dyn = {}

    st = contextlib.ExitStack()
    with st:
        def sb(name, shape, dt):
            return st.enter_context(nc.sbuf_tensor(name, list(shape), dt))

        SC = sb("SC", [128, SC_END], F32)
        BG = sb("BG", [128, 1024], F32)
        TA = sb("TA", [128, TAB_A_END], BF16)
        IDF = sb("IDF", [128, 128], F32)
        GATE = sb("GATE", [128, 2, 1024], F32)
        MODT = sb("MODT", [128, 16, 2], F32)
        G1 = sb("G1", [128, 8, 2], F32)
        SILC = sb("SILC", [128, 16], F32)
        SILCB = sb("SILCB", [128, 16], BF16)
        OT = sb("OT", [128, 8, NTOK], BF16)
        QT = sb("QT", [128, 2, NTOK], BF16)
        TT = sb("TT", [128, 2, 2308], BF16)
        UP = sb("UP", [128, NT, 256], BF16)
        UCSC = sb("UCSC", [128, 2, 512], BF16)
        PWP = sb("PWP", [64, 4, 128], BF16)
        WF = sb("WF", [128, 2, 256], BF16)
        SMALL = sb("SMALL", [128, 64], F32)
        NHALF = sb("NHALF", [128, 8], F32)
        ONES = sb("ONES", [128, 128], F32)
        EDG = sb("EDG", [128, 4], BF16)
        ARENA = sb("ARENA", [128, ARENA_BYTES], U8)
        PS = st.enter_context(nc.psum_tensor("PS", [128, 8, 512], F32))

        cursor = [0]

        def arena_reset():
            cursor[0] = 0

        def av(shape, dt):
            n = int(np.prod(shape[1:])) * (2 if dt == BF16 else 4)
            n = (n + 63) // 64 * 64
            off = cursor[0]
            cursor[0] += n
            assert cursor[0] <= ARENA_BYTES, ("arena overflow", cursor[0])
            v = ARENA[0:shape[0], off:off + int(np.prod(shape[1:])) * (2 if dt == BF16 else 4)].bitcast(dt)
            if len(shape) == 3:
                v = v.rearrange("p (a b) -> p a b", b=shape[2])
            elif len(shape) == 4:
                v = v.rearrange("p (a b c) -> p a b c", b=shape[2], c=shape[3])
            return v

        P = Prog(nc)
        psb = lambda b: "ps%d" % b

        P.dma(SP, TA[:], ta_in, w=["TA"])
        P.dma(SP, IDF[:], idf_in, w=["IDF"])
        P.op(DVE, lambda e: e.memset(NHALF[:], -0.5), w=["NHALF"])
        P.op(DVE, lambda e: e.memset(ONES[:], 1.0), w=["ONES"])
        P.op(DVE, lambda e: e.memset(TT[:], 0.0), w=["TT"])

        def ld_regs(e):
            for r in range(4):
                reg = e.alloc_register("bo%d" % r)
                e.reg_load(reg, boff_in[0:1, r:r + 1])
                dyn[r] = e.snap(reg, min_val=0, max_val=7)
        P.op(SP, ld_regs)
        finals = []
        for l in range(NL):
            sc_in, bg_in, wmod_in, win_in = sc_ins[l], bg_ins[l], wmod_ins[l], win_ins[l]
            pwp_in, wf_in, wo_in = pwp_ins[l], wf_ins[l], wo_ins[l]
            x_in = x_in0 if l == 0 else x1_d
            xc_in = xc_in0 if l == 0 else xc1_d
            last = (l == NL - 1) and not DBG
            if l > 0:
                P.barrier(SMALL[:, 63:64])
            P.dma(SP, SC[:], sc_in, w=["SC"])
            P.dma(SP, BG[:], bg_in, w=["BG"])
            P.dma(POOL, PWP[:], pwp_in, w=["PWP"])
            P.dma(POOL, WF[:], wf_in.rearrange("(c p) n -> p c n", p=128), w=["WF"])

            arena_reset()
            ROPE = av([128, 2, NT, 64], F32)
            HT = av([128, 8, NTOK], BF16)
            WG = [av([128, 8, 512], BF16) for _ in range(3)]
            XT = [av([128, 1024], F32) for _ in range(2)]
            JUNK = av([128, 1024], BF16)
            SREP = av([128, 2, 8, 128], BF16)
            EA = [av([128, 256], F32) for _ in range(2)]
            EB = [av([128, 256], F32) for _ in range(2)]
            QR = [av([128, 256], BF16) for _ in range(2)]
            KR = [av([128, 128], BF16) for _ in range(2)]
            UFT = av([128, 2, 512], BF16)
            UCS = [av([128, 512], BF16) for _ in range(2)]
            KTS = av([128, NTOK], BF16)
            VS = av([128, NT, 128], BF16)
            CT1 = [av([128, 512], F32) for _ in range(2)]
            CT2 = [av([128, 512], F32) for _ in range(2)]
            arena_a_end = cursor[0]

            P.dma(SP, ROPE[:], rope_in, w=["ROPE"])

            P.op(ACT, lambda e: e.activation(out=SILC[:], in_=SC[:, SC_C:SC_C + 16], func=AF.Silu), r=["SC"], w=["SILC"])
            P.op(DVE, lambda e: e.tensor_copy(out=SILCB[:], in_=SILC[:]), r=["SILC"], w=["SILCB"])
            for v in range(2):
                for k in range(8):
                    P.op(DVE, lambda e, v=v, k=k: e.tensor_scalar(
                        out=SREP[:, v, k, :], in0=ONES[:], scalar1=SILC[:, 2 * k + v:2 * k + v + 1], scalar2=None,
                        op0=ALU.mult), r=["SILC", "ONES"], w=["SREP"])
            wmod_v = wmod_in.rearrange("(k p) n -> p k n", p=128)
            for piece in range(6):
                wg = WG[piece % 2]
                wk = "WG%d" % (piece % 2)
                P.dma(POOL, wg[:], wmod_v[:, :, piece * 512:(piece + 1) * 512], w=[wk])
                if piece < 4:
                    for oc in range(4):
                        ch = piece * 4 + oc
                        for k in range(8):
                            P.op(PE, lambda e, wg=wg, oc=oc, k=k: e.matmul(
                                PS[:, 0, oc * 2:oc * 2 + 2], lhsT=wg[:, k, oc * 128:(oc + 1) * 128],
                                rhs=SILCB[:, 2 * k:2 * k + 2], start=(k == 0), stop=(k == 7)),
                                r=[wk, "SILCB"], w=[psb(0)])
                        P.op(DVE, lambda e, oc=oc, ch=ch: e.tensor_scalar(
                            out=MODT[:, ch, :], in0=PS[:, 0, oc * 2:oc * 2 + 2], scalar1=SC[:, SC_BMOD + ch:SC_BMOD + ch + 1],
                            scalar2=None, op0=ALU.add), r=["SC"], w=["MODT", psb(0)])
                else:
                    half = piece - 4
                    for v in range(2):
                        for k in range(8):
                            P.op(PE, lambda e, wg=wg, v=v, k=k: e.matmul(
                                PS[:, 1 + v, :], lhsT=SREP[:, v, k, :], rhs=wg[:, k, :], start=(k == 0), stop=(k == 7)),
                                r=[wk, "SREP"], w=[psb(1 + v)])
                        P.op(DVE, lambda e, v=v, half=half: e.tensor_tensor(
                            out=GATE[:, v, half * 512:(half + 1) * 512], in0=PS[:, 1 + v, :],
                            in1=BG[:, half * 512:(half + 1) * 512], op=ALU.add), r=["BG"], w=["GATE", psb(1 + v)])
            for v in range(2):
                P.op(DVE, lambda e, v=v: e.scalar_tensor_tensor(
                    out=G1[:, :, v], in0=MODT[:, 8:16, v], scalar=1.0, in1=SC[:, SC_G:SC_G + 8],
                    op0=ALU.add, op1=ALU.mult), r=["MODT", "SC"], w=["G1"])

            win_v = win_in.rearrange("(k p) n -> p k n", p=128)
            wg_count = [6]

            def load_group(g):
                i = wg_count[0]
                wg_count[0] += 1
                wg = WG[i % 3]
                P.dma(POOL, wg[:, :, 0:GW[g]], win_v[:, :, GOFF[g]:GOFF[g] + GW[g]], w=["WG%d" % (i % 3)])
                return wg, "WG%d" % (i % 3)

            def tile_src(t):
                if t < 16:
                    return x_in[t * 128:(t + 1) * 128, :], 0
                return xc_in[(t - 16) * 128:(t - 15) * 128, :], 1

            def step1(t):
                src, v = tile_src(t)
                xt = XT[t % 2]
                xn = xt
                xk = "XT%d" % (t % 2)
                nk = xk
                col = t % 32
                P.dma(SP, xt[:], src, w=[xk])
                P.op(ACT, lambda e: e.activation(
                    out=JUNK[:], in_=xt[:], func=AF.Square, accum_out=SMALL[:, col:col + 1]),
                    r=[xk], w=["JUNK", "SM%d" % col])
                P.op(DVE, lambda e: e.tensor_scalar(
                    out=SMALL[:, col:col + 1], in0=SMALL[:, col:col + 1], scalar1=1.0 / D, scalar2=EPS,
                    op0=ALU.mult, op1=ALU.add), w=["SM%d" % col])
                P.op(POOL, lambda e: e.tensor_tensor(
                    out=SMALL[:, col:col + 1], in0=SMALL[:, col:col + 1], in1=NHALF[:, 0:1], op=ALU.pow),
                    r=["NHALF"], w=["SM%d" % col])
                P.op(DVE, lambda e: e.tensor_scalar(
                    out=xn[:], in0=xt[:], scalar1=SMALL[:, col:col + 1], scalar2=None, op0=ALU.mult),
                    r=["SM%d" % col], w=[nk])
                for half in range(2):
                    bank = half
                    for kk in range(4):
                        k = half * 4 + kk
                        P.op(PE, lambda e, k=k, kk=kk, bank=bank: e.transpose(
                            out=PS[:, bank, kk * 128:(kk + 1) * 128], in_=xn[:, k * 128:(k + 1) * 128], identity=IDF[:]),
                            r=[nk, "IDF"], w=[psb(bank)])
                    for kk in range(4):
                        k = half * 4 + kk
                        P.op(ACT, lambda e, k=k, kk=kk, bank=bank: e.activation(
                            out=HT[:, k, t * 128:(t + 1) * 128], in_=PS[:, bank, kk * 128:(kk + 1) * 128],
                            func=AF.Identity, scale=G1[:, k, v:v + 1], bias=MODT[:, k, v:v + 1]),
                            r=["G1", "MODT"], w=["HT%d" % t, psb(bank)])

            def qk_epilogue(items, nh, gain_off, gains_off):
                w = nh * 64
                v3 = lambda a: a.rearrange("p (h d) -> p h d", d=64)
                ctx_ = []
                for idx, (ps_ap, t, out_ap, key_out, bank) in enumerate(items):
                    col = 32 + (t % 2) * 8
                    ss = SMALL[:, col:col + nh]
                    ctx_.append(dict(
                        ps=ps_ap, t=t, out=out_ap, ko=key_out, bank=bank, col=col, ss=ss,
                        ea=EA[t % 2][:, 0:w], eb=EB[t % 2][:, 0:w], ka="EA%d" % (t % 2), kb="EB%d" % (t % 2),
                        ssb=ss.unsqueeze(2).to_broadcast([128, nh, 64]),
                        cosb=ROPE[:, 0, t, :].unsqueeze(1).to_broadcast([128, nh, 64]),
                        sinb=ROPE[:, 1, t, :].unsqueeze(1).to_broadcast([128, nh, 64])))
                for c in ctx_:
                    P.op(ACT, lambda e, c=c: e.activation(out=c["ea"], in_=c["ps"][:, 0:w], func=AF.Square),
                         w=[c["ka"], psb(c["bank"])])
                for c in ctx_:
                    P.op(DVE, lambda e, c=c: e.tensor_reduce(out=c["ss"], in_=v3(c["ea"]), axis=AX.X, op=ALU.add),
                         r=[c["ka"]], w=["SS%d" % c["col"]])
                for c in ctx_:
                    P.op(DVE, lambda e, c=c: e.tensor_scalar(out=c["ss"], in0=c["ss"], scalar1=1.0 / 64, scalar2=EPS,
                                                            op0=ALU.mult, op1=ALU.add), w=["SS%d" % c["col"]])
                for c in ctx_:
                    P.op(ACT, lambda e, c=c: e.activation(out=c["ss"], in_=c["ss"], func=AF.Sqrt), w=["SS%d" % c["col"]])
                for c in ctx_:
                    P.op(DVE, lambda e, c=c: e.reciprocal(out=c["ss"], in_=c["ss"]), w=["SS%d" % c["col"]])
                for c in ctx_:
                    P.op(DVE, lambda e, c=c: e.tensor_tensor(out=v3(c["ea"]), in0=v3(c["ps"][:, 0:w]), in1=c["ssb"], op=ALU.mult),
                         r=["SS%d" % c["col"]], w=[c["ka"], psb(c["bank"])])
                for c in ctx_:
                    P.op(DVE, lambda e, c=c: e.tensor_tensor(out=v3(c["eb"]), in0=v3(c["ps"][:, w:2 * w]), in1=c["ssb"], op=ALU.mult),
                         r=["SS%d" % c["col"]], w=[c["kb"], psb(c["bank"])])
                for c in ctx_:
                    P.op(DVE, lambda e, c=c: e.tensor_tensor(out=c["ea"], in0=c["ea"], in1=SC[:, gain_off:gain_off + w], op=ALU.mult),
                         r=["SC"], w=[c["ka"]])
                for c in ctx_:
                    P.op(DVE, lambda e, c=c: e.tensor_tensor(out=c["eb"], in0=c["eb"], in1=SC[:, gains_off:gains_off + w], op=ALU.mult),
                         r=["SC"], w=[c["kb"]])
                for c in ctx_:
                    P.op(DVE, lambda e, c=c: e.tensor_tensor(out=v3(c["ea"]), in0=v3(c["ea"]), in1=c["cosb"], op=ALU.mult),
                         r=["ROPE"], w=[c["ka"]])
                for c in ctx_:
                    P.op(DVE, lambda e, c=c: e.tensor_tensor(out=v3(c["eb"]), in0=v3(c["eb"]), in1=c["sinb"], op=ALU.mult),
                         r=["ROPE"], w=[c["kb"]])
                for c in ctx_:
                    P.op(DVE, lambda e, c=c: e.tensor_tensor(out=c["out"], in0=c["ea"], in1=c["eb"], op=ALU.add),
                         r=[c["ka"], c["kb"]], w=[c["ko"]])

            TB_TOK = [(0, 512), (512, 512), (1024, 512), (1536, 512), (2048, 256)]
            PSB16 = [PS[:, b, :].bitcast(BF16) for b in range(8)]

            def fm_group(wg, wk, blk, ncc, pb0=4):
                t0, n = TB_TOK[blk]
                for cc in range(ncc):
                    for k in range(8):
                        P.op(PE, lambda e, k=k, cc=cc, wg=wg: e.matmul(
                            PS[:, pb0 + cc, 0:n], lhsT=wg[:, k, cc * 128:(cc + 1) * 128], rhs=HT[:, k, t0:t0 + n],
                            start=(k == 0), stop=(k == 7)),
                            r=[wk] + ["HT%d" % t for t in range(t0 // 128, (t0 + n) // 128)], w=[psb(pb0 + cc)])

            def tt_off(blk):
                t0, n = TB_TOK[blk]
                return (1 + t0) if blk < 4 else 2051

            wg, wk = load_group(1)
            for blk in range(5):
                tiles = list(range(4 * blk, min(4 * blk + 4, NT)))
                for t in tiles:
                    step1(t)
                for pr in range(0, len(tiles), 2):
                    pair = tiles[pr:pr + 2]
                    for t in pair:
                        bank = 2 + (t % 2)
                        for k in range(8):
                            P.op(PE, lambda e, k=k, t=t, bank=bank, wg=wg: e.matmul(
                                PS[:, bank, 0:384], lhsT=HT[:, k, t * 128:(t + 1) * 128], rhs=wg[:, k, 0:384],
                                start=(k == 0), stop=(k == 7)), r=[wk, "HT%d" % t], w=[psb(bank)])
                    qk_epilogue([(PS[:, 2 + (t % 2), 0:256], t, KR[t % 2][:], "KR%d" % (t % 2), 2 + (t % 2)) for t in pair],
                                2, SC_KG, SC_KGS)
                    for t in pair:
                        bank = 2 + (t % 2)
                        kr = KR[t % 2]
                        P.op(ACT, lambda e, t=t, bank=bank: e.activation(out=VS[:, t, :], in_=PS[:, bank, 256:384], func=AF.Copy),
                             w=["VS", psb(bank)])
                        tb_ = 4 + (t % 2)
                        P.op(PE, lambda e, tb_=tb_, kr=kr: e.transpose(
                            out=PSB16[tb_][:, 0:128], in_=kr[:], identity=TA[:, TAB_ID:TAB_ID + 128]),
                            r=["KR%d" % (t % 2), "TA"], w=[psb(tb_)])
                        P.op(ACT, lambda e, tb_=tb_, t=t: e.activation(
                            out=KTS[:, t * 128:(t + 1) * 128], in_=PSB16[tb_][:, 0:128], func=AF.Copy), w=["KTS", psb(tb_)])
            pubkeys = []

            def pubdma(out, in_, r):
                k = "pub%d" % len(pubkeys)
                pubkeys.append(k)
                o_ = P.dma(SP, out, in_, r=r, w=[k])
                if DBG:
                    finals.append(o_)

            pubdma(pK, KTS[:, 0:TLOC], ["KTS"])
            pubdma(pV.rearrange("(t p) n -> p t n", p=128), VS[:, 0:16, :], ["VS"])
            P.dma(SP, cK, KTS[:, TLOC:NTOK], r=["KTS"], w=["cK"])
            P.dma(SP, cV.rearrange("(t p) n -> p t n", p=128), VS[:, 16:18, :], r=["VS"], w=["cV"])

            wg, wk = load_group(6)
            for blk in range(5):
                t0, n = TB_TOK[blk]
                fm_group(wg, wk, blk, 4)
                for cc in range(2):
                    P.op(ACT, lambda e, n=n, cc=cc: e.activation(out=UFT[:, cc, 0:n], in_=PS[:, 4 + cc, 0:n], func=AF.Copy),
                         w=["UFT", psb(4 + cc)])
                for cc in range(2):
                    P.op(ACT, lambda e, n=n, cc=cc, t0=t0: e.activation(
                        out=OT[:, 6 + cc, t0:t0 + n], in_=PS[:, 6 + cc, 0:n], func=AF.Silu),
                        w=["OT%d" % (6 + cc), psb(6 + cc)])
                for ti in range(n // 128):
                    t = t0 // 128 + ti
                    bank = 2 + (t % 2)
                    for cc in range(2):
                        P.op(PE, lambda e, cc=cc, ti=ti, bank=bank: e.matmul(
                            PS[:, bank, :], lhsT=UFT[:, cc, ti * 128:(ti + 1) * 128],
                            rhs=TA[:, TAB_CCS + cc * 512:TAB_CCS + (cc + 1) * 512], start=(cc == 0), stop=(cc == 1)),
                            r=["UFT", "TA"], w=[psb(bank)])
                    if t < 16:
                        ucs = UCS[t % 2]
                        uk = "UCS%d" % (t % 2)
                        P.op(DVE, lambda e, bank=bank, ucs=ucs: e.tensor_copy(out=ucs[:], in_=PS[:, bank, :]),
                             w=[uk, psb(bank)])
                        for reim in range(2):
                            pubdma(pF[:, reim, t * 128:(t + 1) * 128, :].rearrange("q p c -> p q c"),
                                   ucs[:, reim * 256:(reim + 1) * 256].rearrange("p (q c) -> p q c", c=64), [uk])
                    else:
                        P.op(DVE, lambda e, bank=bank, t=t: e.tensor_copy(out=UCSC[:, t - 16, :], in_=PS[:, bank, :]),
                             w=["UCSC", psb(bank)])

            wg, wk = load_group(2)
            for t in range(NT):
                bank = 2 + (t % 2)
                for k in range(8):
                    P.op(PE, lambda e, k=k, t=t, bank=bank, wg=wg: e.matmul(
                        PS[:, bank, 0:256], lhsT=HT[:, k, t * 128:(t + 1) * 128], rhs=wg[:, k, 0:256],
                        start=(k == 0), stop=(k == 7)), r=[wk, "HT%d" % t], w=[psb(bank)])
                P.op(ACT, lambda e, t=t, bank=bank: e.activation(out=UP[:, t, :], in_=PS[:, bank, 0:256], func=AF.Copy),
                     w=["UP", psb(bank)])
            pubdma(pUP[0], UP[:, 0, :], ["UP"])
            pubdma(pUP[1], UP[:, 15, :], ["UP"])

            for c in range(2):
                wg, wk = load_group(3 + c)
                for blk in range(5):
                    t0, n = TB_TOK[blk]
                    pb = 4 * (blk % 2)
                    fm_group(wg, wk, blk, 4, pb)
                    ct1, ct2 = CT1[blk % 2], CT2[blk % 2]
                    k1, k2 = "CT1%d" % (blk % 2), "CT2%d" % (blk % 2)
                    P.op(ACT, lambda e, n=n, ct1=ct1, pb=pb: e.activation(out=ct1[:, 0:n], in_=PS[:, pb, 0:n], func=AF.Copy),
                         w=[k1, psb(pb)])
                    o = tt_off(blk)
                    P.op(DVE, lambda e, n=n, ct1=ct1, o=o, c=c, pb=pb: e.tensor_tensor(
                        out=TT[:, c, o:o + n], in0=PS[:, pb + 1, 0:n], in1=ct1[:, 0:n], op=ALU.mult),
                        r=[k1], w=["TT", psb(pb + 1)])
                    P.op(ACT, lambda e, n=n, ct2=ct2, pb=pb: e.activation(out=ct2[:, 0:n], in_=PS[:, pb + 3, 0:n], func=AF.Silu),
                         w=[k2, psb(pb + 3)])
                    P.op(DVE, lambda e, n=n, ct2=ct2, c=c, t0=t0, pb=pb: e.tensor_tensor(
                        out=OT[:, 4 + c, t0:t0 + n], in0=PS[:, pb + 2, 0:n], in1=ct2[:, 0:n], op=ALU.mult),
                        r=[k2], w=["OT%d" % (4 + c), psb(pb + 2)])
            for c in range(2):
                P.op(DVE, lambda e, c=c: e.tensor_copy(out=EDG[:, 2 * c:2 * c + 1], in_=TT[:, c, 1:2]), r=["TT"], w=["EDG"])
                P.op(DVE, lambda e, c=c: e.tensor_copy(out=EDG[:, 2 * c + 1:2 * c + 2], in_=TT[:, c, 2048:2049]),
                     r=["TT"], w=["EDG"])
            pubdma(pT, EDG[:], ["EDG"])

            pre_g5 = load_group(5)
            pre_g0 = load_group(0)
            if not _NO_CC and not DBG:
                P.cc(lambda e: e.collective_compute("AllGather", ALU.bypass, replica_groups=[list(range(8))],
                                                    ins=[pub.opt()], outs=[gath.opt()]), r=list(pubkeys), w=["gath"])
            for r_ in range(4):
                P.op(SP, lambda e, r_=r_: e.dma_start(out=gloc[r_:r_ + 1, :, :], in_=gath3[bass.ds(dyn[r_], 1), :, :]),
                     r=["gath"], w=["gloc%d" % r_], dma=True)
            GL = ["gloc%d" % r_ for r_ in range(4)]

            wg, wk = pre_g5
            for blk in range(5):
                t0, n = TB_TOK[blk]
                pb = 4 * (blk % 2)
                fm_group(wg, wk, blk, 4, pb)
                for cc in range(4):
                    dst = (2 + cc) if cc < 2 else (cc - 2)
                    P.op(ACT, lambda e, n=n, cc=cc, dst=dst, t0=t0, pb=pb: e.activation(
                        out=OT[:, dst, t0:t0 + n], in_=PS[:, pb + cc, 0:n], func=AF.Silu),
                        w=["OT%d" % dst, psb(pb + cc)])

            wg, wk = pre_g0
            for pr in range(0, NT, 2):
                pair = [pr, pr + 1]
                for t in pair:
                    bank = 2 + (t % 2)
                    for k in range(8):
                        P.op(PE, lambda e, k=k, t=t, bank=bank, wg=wg: e.matmul(
                            PS[:, bank, :], lhsT=HT[:, k, t * 128:(t + 1) * 128], rhs=wg[:, k, 0:512],
                            start=(k == 0), stop=(k == 7)), r=[wk, "HT%d" % t], w=[psb(bank)])
                qk_epilogue([(PS[:, 2 + (t % 2), :], t, QR[t % 2][:], "QR%d" % (t % 2), 2 + (t % 2)) for t in pair],
                            4, SC_QG, SC_QGS)
                for t in pair:
                    qr = QR[t % 2]
                    tb_ = 4 + (t % 2)
                    for gi in range(2):
                        P.op(PE, lambda e, gi=gi, tb_=tb_, qr=qr: e.transpose(
                            out=PSB16[tb_][:, gi * 128:(gi + 1) * 128], in_=qr[:, gi * 128:(gi + 1) * 128],
                            identity=TA[:, TAB_ID:TAB_ID + 128]), r=["QR%d" % (t % 2), "TA"], w=[psb(tb_)])
                    P.op(ACT, lambda e, tb_=tb_, t=t: e.activation(
                        out=QT[:, :, t * 128:(t + 1) * 128],
                        in_=PSB16[tb_][:, 0:256].rearrange("p (g n) -> p g n", n=128), func=AF.Copy),
                        w=["QT", psb(tb_)])

            P.barrier(SMALL[:, 63:64])
            arena_reset()
            TB = av([128, TB_END], BF16)
            WO = av([128, 8, 1024], BF16)
            keep_off = cursor[0]
            XG = av([128, 128, 64], BF16)
            BQ = av([128, 64, 128], BF16)
            YT = av([128, 2, NTOK], BF16)
            EUP = av([128, 8, 256], BF16)
            HUP = av([128, 2, 256], BF16)
            ET = av([128, 4, 4], BF16)
            ETF = av([128, 4, 4], F32)
            HTF = av([128, 4], F32)
            CA = [av([128, 512], F32) for _ in range(2)]
            DT = [av([64, 4, 128], BF16) for _ in range(2)]
            P.dma(SP, TB[:], tb_in, w=["TB"])
            P.dma(POOL, WO[:], wo_in.rearrange("(k p) n -> p k n", p=128), w=["WO"])
            for r_ in range(4):
                P.dma(SP, EUP[:, 2 * r_:2 * r_ + 2, :], gUP[r_].rearrange("s p c -> p s c"), r=GL, w=["EUP"])
            P.dma(SP, ET[:], gT.rearrange("r p f -> p r f"), r=GL, w=["ET"])
            P.op(DVE, lambda e: e.tensor_copy(out=ETF[:], in_=ET[:]), r=["ET"], w=["ETF"])
            for c in range(2):
                for side in range(2):
                    srccol = 2 * c + 1 if side == 0 else 2 * c
                    sel = SC_SELL if side == 0 else SC_SELR
                    hcol = 2 * c + side
                    P.op(DVE, lambda e, srccol=srccol, sel=sel, hcol=hcol: e.tensor_scalar(
                        out=HTF[:, hcol:hcol + 1], in0=ETF[:, 0, srccol:srccol + 1], scalar1=SC[:, sel:sel + 1],
                        scalar2=None, op0=ALU.mult), r=["ETF", "SC"], w=["HTF"])
                    for r_ in range(1, 4):
                        P.op(DVE, lambda e, srccol=srccol, sel=sel, hcol=hcol, r_=r_: e.scalar_tensor_tensor(
                            out=HTF[:, hcol:hcol + 1], in0=ETF[:, r_, srccol:srccol + 1],
                            scalar=SC[:, sel + r_:sel + r_ + 1], in1=HTF[:, hcol:hcol + 1], op0=ALU.mult, op1=ALU.add),
                            r=["ETF", "SC"], w=["HTF"])
                    dstcol = 0 if side == 0 else 2049
                    P.op(DVE, lambda e, c=c, hcol=hcol, dstcol=dstcol: e.tensor_copy(
                        out=TT[:, c, dstcol:dstcol + 1], in_=HTF[:, hcol:hcol + 1]), r=["HTF"], w=["TT"])
            for side in range(2):
                sel = SC_SELL if side == 0 else SC_SELR
                s_idx = 1 if side == 0 else 0
                P.op(DVE, lambda e, side=side, sel=sel, s_idx=s_idx: e.tensor_scalar(
                    out=HUP[:, side, :], in0=EUP[:, s_idx, :], scalar1=SC[:, sel:sel + 1], scalar2=None, op0=ALU.mult),
                    r=["EUP", "SC"], w=["HUP"])
                for r_ in range(1, 4):
                    P.op(DVE, lambda e, side=side, sel=sel, s_idx=s_idx, r_=r_: e.scalar_tensor_tensor(
                        out=HUP[:, side, :], in0=EUP[:, 2 * r_ + s_idx, :], scalar=SC[:, sel + r_:sel + r_ + 1],
                        in1=HUP[:, side, :], op0=ALU.mult, op1=ALU.add), r=["EUP", "SC"], w=["HUP"])

            for c in range(2):
                for blk in range(5):
                    t0, n = TB_TOK[blk]
                    o = tt_off(blk)
                    ca = CA[blk % 2]
                    ck = "CA%d" % (blk % 2)
                    cwl = [SC[:, SC_CW + c * 3 + tap:SC_CW + c * 3 + tap + 1] for tap in range(3)]
                    cw = lambda tap, cwl=cwl: cwl[tap]
                    P.op(DVE, lambda e, n=n, o=o, ca=ca, c=c, cw=cw: e.tensor_scalar(
                        out=ca[:, 0:n], in0=TT[:, c, o - 1:o - 1 + n], scalar1=cw(0), scalar2=None, op0=ALU.mult),
                        r=["TT", "SC"], w=[ck])
                    for tap in (1, 2):
                        P.op(DVE, lambda e, n=n, o=o, ca=ca, c=c, cw=cw, tap=tap: e.scalar_tensor_tensor(
                            out=ca[:, 0:n], in0=TT[:, c, o - 1 + tap:o - 1 + tap + n], scalar=cw(tap), in1=ca[:, 0:n],
                            op0=ALU.mult, op1=ALU.add), r=["TT", "SC"], w=[ck])
                    P.op(DVE, lambda e, n=n, ca=ca, c=c, t0=t0: e.scalar_tensor_tensor(
                        out=OT[:, 4 + c, t0:t0 + n], in0=ca[:, 0:n], scalar=SC[:, SC_CB + c:SC_CB + c + 1],
                        in1=OT[:, 4 + c, t0:t0 + n], op0=ALU.add, op1=ALU.mult), r=[ck, "SC"], w=["OT%d" % (4 + c)])

            def ptab(wi, kind):
                o = TB_POOL + (wi * 7 + kind) * 128
                return TB[:, o:o + 128]

            for t in range(NT):
                bank = 2 + (t % 2)
                if t < 16:
                    prev = HUP[:, 0, :] if t == 0 else UP[:, t - 1, :]
                    nxt = HUP[:, 1, :] if t == 15 else UP[:, t + 1, :]
                    ckind = 3 if t == 0 else (4 if t == 15 else 2)
                else:
                    prev = UP[:, 16, :] if t == 17 else None
                    nxt = UP[:, 17, :] if t == 16 else None
                    ckind = 5 if t == 16 else 6
                for g in range(4):
                    srcs = [(UP[:, t, :], ckind)]
                    if prev is not None:
                        srcs.append((prev, 0))
                    if nxt is not None:
                        srcs.append((nxt, 1))
                    for si, (src, kind) in enumerate(srcs):
                        P.op(PE, lambda e, g=g, src=src, kind=kind, si=si, ns=len(srcs), bank=bank: e.matmul(
                            PS[0:64, bank, g * 128:(g + 1) * 128], lhsT=src[:, g * 64:(g + 1) * 64], rhs=ptab(g, kind),
                            start=(si == 0), stop=(si == ns - 1)), r=["UP", "HUP", "TB"], w=[psb(bank)])
                dt_ = DT[t % 2]
                dk = "DT%d" % (t % 2)
                P.op(ACT, lambda e, dt_=dt_, bank=bank: e.activation(
                    out=dt_[:], in_=PS[0:64, bank, :].rearrange("p (g n) -> p g n", n=128), func=AF.Copy),
                    w=[dk, psb(bank)])
                b2 = 4 + (t % 2)
                for c in range(2):
                    for gg in range(2):
                        g = 2 * c + gg
                        P.op(PE, lambda e, c=c, g=g, gg=gg, dt_=dt_, b2=b2: e.matmul(
                            PS[:, b2, c * 128:(c + 1) * 128], lhsT=PWP[:, g, :], rhs=dt_[:, g, :],
                            start=(gg == 0), stop=(gg == 1)), r=[dk, "PWP"], w=[psb(b2)])
                for c in range(2):
                    P.op(DVE, lambda e, c=c, t=t, b2=b2: e.scalar_tensor_tensor(
                        out=OT[:, 6 + c, t * 128:(t + 1) * 128], in0=PS[:, b2, c * 128:(c + 1) * 128],
                        scalar=SC[:, SC_PS + c:SC_PS + c + 1], in1=OT[:, 6 + c, t * 128:(t + 1) * 128],
                        op0=ALU.mult, op1=ALU.mult), r=["SC"], w=["OT%d" % (6 + c), psb(b2)])

            for q in range(4):
                for r_ in range(4):
                    for reim in range(2):
                        P.dma(SP, XG[reim * 64 + r_ * 16:reim * 64 + r_ * 16 + 16, :, :],
                              gF[r_][q, reim].rearrange("(i n) c -> i n c", n=128), r=GL, w=["XG%d" % (reim * 4 + r_)])
                for c4 in range(16):
                    bank = c4 % 2
                    for ci in range(4):
                        ch = c4 * 4 + ci
                        P.op(PE, lambda e, ch=ch, ci=ci, bank=bank: e.matmul(
                            PS[:, bank, ci * 128:(ci + 1) * 128], lhsT=XG[:, :, ch], rhs=TA[:, TAB_W1:TAB_W1 + 128],
                            start=True, stop=True), r=["XG%d" % i for i in range(8)] + ["TA"], w=[psb(bank)])
                    eng = ACT if c4 % 2 == 0 else DVE
                    if eng == ACT:
                        P.op(ACT, lambda e, c4=c4, bank=bank: e.activation(
                            out=BQ[:, c4 * 4:(c4 + 1) * 4, :], in_=PS[:, bank, :].rearrange("p (a b) -> p a b", b=128),
                            func=AF.Copy), w=["BQ", psb(bank)])
                    else:
                        P.op(DVE, lambda e, c4=c4, bank=bank: e.tensor_copy(
                            out=BQ[:, c4 * 4:(c4 + 1) * 4, :], in_=PS[:, bank, :].rearrange("p (a b) -> p a b", b=128)),
                            w=["BQ", psb(bank)])
                for k2 in range(64):
                    bank = 4 + k2 // 16
                    oc = (k2 % 16) * 32
                    P.op(PE, lambda e, k2=k2, bank=bank, oc=oc: e.matmul(
                        PS[0:64, bank, oc:oc + 32], lhsT=BQ[:, :, k2], rhs=TB[:, TB_T2C + k2 * 32:TB_T2C + (k2 + 1) * 32],
                        start=True, stop=False), r=["BQ", "TB"], w=[psb(bank)])
                    P.op(PE, lambda e, k2=k2, bank=bank, oc=oc: e.matmul(
                        PS[0:64, bank, oc:oc + 32], lhsT=BQ[:, :, 64 + k2], rhs=TB[:, TB_T2S + k2 * 32:TB_T2S + (k2 + 1) * 32],
                        start=False, stop=True), r=["BQ", "TB"], w=[psb(bank)])
                for bi in range(4):
                    po = (q % 2) * 64
                    dst = YT[po:po + 64, q // 2, 0:TLOC].rearrange("p (k1 k2) -> p k2 k1", k2=64)[:, bi * 16:(bi + 1) * 16, :]
                    src = PS[0:64, 4 + bi, :].rearrange("p (k2 k1) -> p k2 k1", k1=32)
                    if bi % 2 == 0:
                        P.op(ACT, lambda e, dst=dst, src=src: e.activation(out=dst, in_=src, func=AF.Copy),
                             w=["YT", psb(4 + bi)])
                    else:
                        P.op(DVE, lambda e, dst=dst, src=src: e.tensor_copy(out=dst, in_=src), w=["YT", psb(4 + bi)])
            for c in range(2):
                idx = 0
                for tl in range(2):
                    for reim in range(2):
                        o = TB_TCC + (tl * 2 + reim) * 256
                        P.op(PE, lambda e, c=c, tl=tl, reim=reim, o=o, idx=idx: e.matmul(
                            PS[:, 2 + c, 0:256], lhsT=UCSC[:, tl, reim * 256 + c * 128:reim * 256 + (c + 1) * 128],
                            rhs=TB[:, o:o + 256], start=(idx == 0), stop=(idx == 3)), r=["UCSC", "TB"], w=[psb(2 + c)])
                        idx += 1
                P.op(ACT, lambda e, c=c: e.activation(out=YT[:, c, TLOC:NTOK], in_=PS[:, 2 + c, 0:256], func=AF.Copy),
                     w=["YT", psb(2 + c)])
            for blk in range(5):
                t0, n = TB_TOK[blk]
                for oc in range(2):
                    bank = 2 * (blk % 2) + oc
                    for ic in range(2):
                        P.op(PE, lambda e, oc=oc, ic=ic, bank=bank, t0=t0, n=n: e.matmul(
                            PS[:, bank, 0:n], lhsT=WF[:, ic, oc * 128:(oc + 1) * 128], rhs=YT[:, ic, t0:t0 + n],
                            start=(ic == 0), stop=(ic == 1)), r=["WF", "YT"], w=[psb(bank)])
                    P.op(DVE, lambda e, oc=oc, bank=bank, t0=t0, n=n: e.tensor_tensor(
                        out=OT[:, oc, t0:t0 + n], in0=PS[:, bank, 0:n], in1=OT[:, oc, t0:t0 + n], op=ALU.mult),
                        w=["OT%d" % oc, psb(bank)])

            P.barrier(SMALL[:, 63:64])
            cursor[0] = keep_off
            KT = av([128, 66 * 128], BF16)
            VA = av([128, 66, 192], BF16)
            PT2 = [av([128, 2, 512], BF16) for _ in range(3)]
            OAB = [[av([128, 512], F32) for _ in range(2)] for _ in range(2)]
            RC = av([128, 512], F32)
            RC2 = av([128, 512], F32)
            TMP = av([128, 512], F32)
            P.dma(SP, KT[:, 0:NCTX], cK, r=["cK"], w=["KT"])
            for r_ in range(4):
                P.dma(SP, KT[:, NCTX + r_ * TLOC:NCTX + (r_ + 1) * TLOC], gK[r_], r=GL, w=["KT"])
            P.op(DVE, lambda e: e.memset(VA[:, :, 64:128], 1.0), w=["VA"])
            for kv in range(2):
                P.dma(SP, VA[:, 0:2, kv * 128:kv * 128 + 64],
                      cV.rearrange("(t p) n -> p t n", p=128)[:, :, kv * 64:(kv + 1) * 64], r=["cV"], w=["VA"])
                for r_ in range(4):
                    P.dma(SP, VA[:, 2 + 16 * r_:2 + 16 * (r_ + 1), kv * 128:kv * 128 + 64],
                          gV[r_].rearrange("(t p) n -> p t n", p=128)[:, :, kv * 64:(kv + 1) * 64], r=GL, w=["VA"])

            items = []
            pairs = 0
            for blk in range(4 if last else 5):
                t0, n = TB_TOK[blk]
                chunks = list(range(66)) if blk < 4 else [0, 1]
                for gi in range(2):
                    for ci, ch in enumerate(chunks):
                        items.append((gi, ci, ch, len(chunks), pairs, t0, n))
                    pairs += 1
            SLOT = [0, 2, 6]
            LOOK = 2

            def rec_S(i):
                gi, ci, ch, nch, pr, t0, n = items[i]
                b0 = SLOT[i % 3]
                for s in range(2):
                    P.op(PE, lambda e, s=s: e.matmul(
                        PS[:, b0 + s, 0:n], lhsT=KT[64 * s:64 * s + 64, ch * 128:(ch + 1) * 128],
                        rhs=QT[64 * s:64 * s + 64, gi, t0:t0 + n], start=True, stop=True),
                        r=["KT", "QT"], w=[psb(b0 + s)])

            def rec_E(i):
                gi, ci, ch, nch, pr, t0, n = items[i]
                b0 = SLOT[i % 3]
                pt = PT2[i % 3]
                P.op(ACT, lambda e: e.activation(
                    out=pt[:, :, 0:n], in_=PS[:, b0:b0 + 2, 0:n], func=AF.Exp, scale=0.125),
                    w=["PT%d" % (i % 3), psb(b0), psb(b0 + 1)])

            def rec_V(i):
                gi, ci, ch, nch, pr, t0, n = items[i]
                pt = PT2[i % 3]
                for s in range(2):
                    P.op(PE, lambda e, s=s: e.matmul(
                        PS[:, 4 + s, 0:n], lhsT=VA[:, ch, 64 * s:64 * s + 128], rhs=pt[:, s, 0:n],
                        start=(ci == 0), stop=(ci == nch - 1)), r=["VA", "PT%d" % (i % 3)], w=[psb(4 + s)])
                if ci != nch - 1:
                    return
                oab = OAB[pr % 2]
                ok_ = ["OAB%d_%d" % (pr % 2, s) for s in range(2)]
                P.op(DVE, lambda e: e.tensor_copy(out=oab[0][:, 0:n], in_=PS[:, 4, 0:n]), w=[ok_[0], psb(4)])
                P.op(ACT, lambda e: e.activation(out=oab[1][:, 0:n], in_=PS[:, 5, 0:n], func=AF.Copy), w=[ok_[1], psb(5)])
                for s in range(2):
                    rs = 64 * (1 - s)
                    os_ = 64 * s
                    o_ = oab[s]
                    P.op(DVE, lambda e, rs=rs, o_=o_: e.reciprocal(out=RC[rs:rs + 64, 0:n], in_=o_[rs:rs + 64, 0:n]),
                         r=[ok_[s]], w=["RC"])
                    P.op(DVE, lambda e, rs=rs, os_=os_: e.tensor_copy(out=RC2[os_:os_ + 64, 0:n], in_=RC[rs:rs + 64, 0:n]),
                         r=["RC"], w=["RC2"])
                    P.op(DVE, lambda e, os_=os_, o_=o_: e.tensor_tensor(
                        out=TMP[os_:os_ + 64, 0:n], in0=o_[os_:os_ + 64, 0:n], in1=RC2[os_:os_ + 64, 0:n], op=ALU.mult),
                        r=["RC2", ok_[s]], w=["TMP"])
                    P.op(DVE, lambda e, os_=os_: e.tensor_tensor(
                        out=OT[os_:os_ + 64, 2 + gi, t0:t0 + n], in0=TMP[os_:os_ + 64, 0:n],
                        in1=OT[os_:os_ + 64, 2 + gi, t0:t0 + n], op=ALU.mult), r=["TMP"], w=["OT%d" % (2 + gi)])

            for i in range(min(LOOK, len(items))):
                rec_S(i)
            for i in range(len(items)):
                rec_E(i)
                rec_V(i)
                if i + LOOK < len(items):
                    rec_S(i + LOOK)

            P.barrier(SMALL[:, 63:64])
            cursor[0] = keep_off
            XR = [av([128, 1024], F32) for _ in range(2)]
            XO = [av([128, 1024], F32) for _ in range(2)]
            for t in range(16 if last else NT):
                src, v = tile_src(t)
                xr, xo = XR[t % 2], XO[t % 2]
                rk, ok = "XR%d" % (t % 2), "XO%d" % (t % 2)
                P.dma(SP, xr[:], src, w=[rk])
                for half in range(2):
                    bank = (2 * t + half) % 4
                    for k in range(8):
                        P.op(PE, lambda e, k=k, t=t, half=half, bank=bank: e.matmul(
                            PS[:, bank, :], lhsT=OT[:, k, t * 128:(t + 1) * 128], rhs=WO[:, k, half * 512:(half + 1) * 512],
                            start=(k == 0), stop=(k == 7)), r=["OT%d" % k, "WO"], w=[psb(bank)])
                    P.op(DVE, lambda e, half=half, bank=bank, v=v, xo=xo: e.tensor_tensor(
                        out=xo[:, half * 512:(half + 1) * 512], in0=PS[:, bank, :], in1=GATE[:, v, half * 512:(half + 1) * 512],
                        op=ALU.mult), r=["GATE"], w=[ok, psb(bank)])
                    P.op(DVE, lambda e, half=half, xo=xo, xr=xr: e.tensor_tensor(
                        out=xo[:, half * 512:(half + 1) * 512], in0=xo[:, half * 512:(half + 1) * 512],
                        in1=xr[:, half * 512:(half + 1) * 512], op=ALU.add), r=[rk], w=[ok])
                if last:
                    dst = x_out[t * 128:(t + 1) * 128, :]
                    finals.append(P.dma(SP, dst, xo[:], r=[ok]))
                else:
                    dst = x1_d[t * 128:(t + 1) * 128, :] if t < 16 else xc1_d[(t - 16) * 128:(t - 15) * 128, :]
                    o_ = P.dma(SP, dst, xo[:], r=[ok], w=["xnext%d" % t])
                    if DBG:
                        finals.append(o_)
        P.emit(final_wait_ops=finals)
    return nc


_PROGS = {}


def _prog():
    if "F" not in _PROGS:
        _PROGS["F"] = build_program("FUSED")
    return _PROGS["F"]


def kernel(**inputs):
    inp = {k: np.asarray(v) for k, v in inputs.items()}
    x = np.asarray(inp["x"], np.float32)
    xc = np.asarray(inp["ctx"], np.float32)
    idf = np.eye(128, dtype=np.float32)
    maps = []
    for core in range(8):
        b, j = core // 4, core % 4
        ta, tb, rope = _tables(j)
        m = {
            "x": np.ascontiguousarray(x[b, j * TLOC:(j + 1) * TLOC]), "xc": np.ascontiguousarray(xc[b]),
            "tab_a": ta, "tab_b": tb, "rope": rope, "idf": idf,
            "boff": (np.arange(4, dtype=np.int32) + 4 * b).reshape(1, 4),
        }
        for l in range(DEPTH):
            for k, v in _layer_host_inputs(l, b, j, inp).items():
                m["%s%d" % (k, l)] = v
        maps.append(m)
    res = run_bass_kernel_spmd(_prog(), maps, core_ids=list(range(8))).results
    out = np.stack([np.concatenate([res[b * 4 + j]["x_out"] for j in range(4)], axis=0) for b in range(2)])
    return out.astype(np.float32)
```
